# Optimizing a Trainium2 kernel written in Bass

```python
import math
import jax, jax.numpy as jnp
from jax import lax
import numpy as np

D_MODEL = 2048
BATCH = 16
SEQ = 2048
DEPTH = 2

HEAD_DIM = 64
HEADS_PER_GROUP = 8
DILATED_GROUPS = ((128, 1), (512, 4), (2048, 16))
N_GROUPS = len(DILATED_GROUPS)
N_ATTN_HEADS = N_GROUPS * HEADS_PER_GROUP
ATTN_WIDTH = N_ATTN_HEADS * HEAD_DIM
ATTN_OUT_WIDTH = HEADS_PER_GROUP * HEAD_DIM
ROPE_THETA = 10000.0
HYENA_WIDTH = D_MODEL // 2
HYENA_ORDER = 2
HYENA_EMB_DIM = 33
HYENA_FILTER_HIDDEN = 64
HYENA_DECAY_TARGET = 1e-2
HYENA_FAST_DECAY = 0.3
HYENA_SLOW_DECAY = 1.5
SHORT_CONV = 3
D_FF = 5632
IN_WIDTH = 3 * ATTN_WIDTH + (HYENA_ORDER + 1) * HYENA_WIDTH
RMS_EPS = 1e-6
MASK_VALUE = -1e30

kernel_name = "hybrid_dilated_attn_hyena_convffn_encoder"


def rmsnorm(x, g):
    xf = x.astype(jnp.float32)
    y = xf * lax.rsqrt(jnp.mean(xf * xf, axis=-1, keepdims=True) + RMS_EPS)
    return (y * g.astype(jnp.float32)).astype(x.dtype)


def dwconv3(x, w, b):
    xp = jnp.pad(x, ((0, 0), (1, 1), (0, 0)))
    return xp[:, :-2] * w[0] + xp[:, 1:-1] * w[1] + xp[:, 2:] * w[2] + b


def rope_tables(S):
    pos = jnp.arange(S, dtype=jnp.float32)
    inv = 1.0 / (ROPE_THETA ** (jnp.arange(0, HEAD_DIM, 2, dtype=jnp.float32) / HEAD_DIM))
    ang = pos[:, None] * inv[None, :]
    ang = jnp.concatenate([ang, ang], axis=-1)
    return jnp.cos(ang), jnp.sin(ang)


def apply_rope(x, cos, sin):
    half = HEAD_DIM // 2
    rot = jnp.concatenate([-x[..., half:], x[..., :half]], axis=-1)
    return x * cos[None, :, None, :] + rot * sin[None, :, None, :]


def dilated_window_attention(q, k, v, window, dilation):
    B, S, H, Dh = q.shape
    r = dilation
    nside = window // (2 * r)
    blk = nside
    T = S // r
    nb = -(-T // blk)
    Tp = nb * blk

    def to_sub(a):
        return a.reshape(B, T, r, H, Dh).transpose(0, 2, 1, 3, 4)

    qs = jnp.pad(to_sub(q), ((0, 0), (0, 0), (0, Tp - T), (0, 0), (0, 0)))
    qb = qs.reshape(B, r, nb, blk, H, Dh)

    def band(a):
        ap = jnp.pad(to_sub(a), ((0, 0), (0, 0), (blk, blk + Tp - T), (0, 0), (0, 0)))
        ab = ap.reshape(B, r, nb + 2, blk, H, Dh)
        return jnp.concatenate([ab[:, :, :-2], ab[:, :, 1:-1], ab[:, :, 2:]], axis=3)

    kw = band(k)
    vw = band(v)
    u = jnp.arange(blk)[:, None]
    s = jnp.arange(3 * blk)[None, :]
    b = jnp.arange(nb)[:, None, None]
    tk = (b - 1) * blk + s
    delta = s - blk - u
    valid = (jnp.abs(delta) <= nside)[None] & (tk >= 0) & (tk < T)
    scores = jnp.einsum('brnqhd,brnkhd->brnhqk', qb * (1.0 / math.sqrt(Dh)), kw)
    scores = jnp.where(valid[None, None, :, None], scores, MASK_VALUE)
    lse = jax.nn.logsumexp(scores, axis=-1)
    p = jnp.exp(scores - lse[..., None])
    o = jnp.einsum('brnhqk,brnkhd->brnqhd', p, vw)
    o = o.reshape(B, r, Tp, H, Dh)[:, :, :T].transpose(0, 2, 1, 3, 4).reshape(B, S, H, Dh)
    lse = lse.transpose(0, 1, 2, 4, 3).reshape(B, r, Tp, H)[:, :, :T]
    lse = lse.transpose(0, 2, 1, 3).reshape(B, S, H)
    return o, lse


def hyena_position_features(L):
    t = jnp.linspace(0.0, 1.0, L, dtype=jnp.float32)[:, None]
    bands = (HYENA_EMB_DIM - 1) // 2
    w = 2.0 * math.pi * jnp.arange(L, dtype=jnp.float32)[:, None] / L
    f = jnp.linspace(1e-4, bands - 1, bands, dtype=jnp.float32)[None, :]
    feats = jnp.concatenate([t, jnp.cos(f * w), -jnp.sin(f * w)], axis=-1)
    return feats, t


def hyena_filter_spectrum(feats, t, f_w1, f_b1, f_freq, f_w2, f_b2, f_w3):
    L = feats.shape[0]
    f32 = jnp.float32
    h = jnp.sin(f_freq[0].astype(f32) * (feats @ f_w1.astype(f32) + f_b1.astype(f32)))
    h = jnp.sin(f_freq[1].astype(f32) * (h @ f_w2.astype(f32) + f_b2.astype(f32)))
    h = (h @ f_w3.astype(f32)).reshape(L, HYENA_ORDER, 2, HYENA_WIDTH)
    max_decay = math.log(HYENA_DECAY_TARGET) / HYENA_FAST_DECAY
    min_decay = math.log(HYENA_DECAY_TARGET) / HYENA_SLOW_DECAY
    deltas = jnp.linspace(min_decay, max_decay, HYENA_WIDTH, dtype=f32)
    decay = jnp.exp(-t * jnp.abs(deltas)[None, :])
    h = h * decay[:, None, None, :]
    fwd, bwd = h[:, :, 0], h[:, :, 1]
    k = jnp.concatenate([fwd, jnp.zeros_like(fwd[:1]), bwd[:0:-1]], axis=0)
    k = k / jnp.sum(jnp.abs(k), axis=0, keepdims=True)
    return jnp.fft.rfft(k, axis=0)


def bidirectional_long_conv(z, kf, skip):
    L = z.shape[1]
    zf = jnp.fft.rfft(z, n=2 * L, axis=1)
    y = jnp.fft.irfft(zf * kf[None], n=2 * L, axis=1)[:, :L]
    return y + z * skip.astype(jnp.float32)


def setup_inputs(seed: int = 0) -> dict:
    key = jax.random.key(seed)
    ks = jax.random.split(key, 24)
    n = lambda k, shape, scale: jax.random.normal(k, shape, jnp.float32) * scale
    L_ = DEPTH
    return {
        "x": n(ks[0], (BATCH, SEQ, D_MODEL), 1.0),
        "attn_norm": 1.0 + n(ks[1], (L_, D_MODEL), 0.02),
        "w_in": n(ks[2], (L_, D_MODEL, IN_WIDTH), D_MODEL ** -0.5),
        "hy_conv_w": n(ks[3], (L_, SHORT_CONV, (HYENA_ORDER + 1) * HYENA_WIDTH), SHORT_CONV ** -0.5),
        "hy_conv_b": n(ks[4], (L_, (HYENA_ORDER + 1) * HYENA_WIDTH), 0.02),
        "f_w1": n(ks[5], (L_, HYENA_EMB_DIM, HYENA_FILTER_HIDDEN), HYENA_EMB_DIM ** -0.5),
        "f_b1": n(ks[6], (L_, HYENA_FILTER_HIDDEN), 0.1),
        "f_freq": 1.0 + n(ks[7], (L_, 2, HYENA_FILTER_HIDDEN), 0.1),
        "f_w2": n(ks[8], (L_, HYENA_FILTER_HIDDEN, HYENA_FILTER_HIDDEN), HYENA_FILTER_HIDDEN ** -0.5),
        "f_b2": n(ks[9], (L_, HYENA_FILTER_HIDDEN), 0.1),
        "f_w3": n(ks[10], (L_, HYENA_FILTER_HIDDEN, HYENA_ORDER * 2 * HYENA_WIDTH), HYENA_FILTER_HIDDEN ** -0.5),
        "hy_skip": n(ks[11], (L_, HYENA_ORDER, HYENA_WIDTH), 0.3),
        "w_proj_attn": n(ks[12], (L_, ATTN_OUT_WIDTH, D_MODEL), ATTN_OUT_WIDTH ** -0.5),
        "w_proj_hyena": n(ks[13], (L_, HYENA_WIDTH, D_MODEL), HYENA_WIDTH ** -0.5),
        "w_gate": n(ks[14], (L_, D_MODEL, 2 * D_MODEL), D_MODEL ** -0.5),
        "b_gate": n(ks[15], (L_, 2 * D_MODEL), 0.02),
        "w_out": n(ks[16], (L_, D_MODEL, D_MODEL), D_MODEL ** -0.5),
        "ffn_norm": 1.0 + n(ks[17], (L_, D_MODEL), 0.02),
        "w_up": n(ks[18], (L_, D_MODEL, 2 * D_FF), D_MODEL ** -0.5),
        "ffn_conv_w": n(ks[19], (L_, SHORT_CONV, D_FF), SHORT_CONV ** -0.5),
        "ffn_conv_b": n(ks[20], (L_, D_FF), 0.02),
        "w_down": n(ks[21], (L_, D_FF, D_MODEL), D_FF ** -0.5),
        "final_norm": 1.0 + n(ks[22], (D_MODEL,), 0.02),
    }


def reference(x, attn_norm, w_in, hy_conv_w, hy_conv_b, f_w1, f_b1, f_freq, f_w2, f_b2, f_w3, hy_skip,
              w_proj_attn, w_proj_hyena, w_gate, b_gate, w_out, ffn_norm, w_up, ffn_conv_w, ffn_conv_b,
              w_down, final_norm):
    B, S, _ = x.shape
    f32 = jnp.float32
    cos, sin = rope_tables(S)
    feats, t = hyena_position_features(S)
    for l in range(DEPTH):
        h = rmsnorm(x, attn_norm[l])
        proj = h @ w_in[l]
        q = proj[..., :ATTN_WIDTH]
        k = proj[..., ATTN_WIDTH:2 * ATTN_WIDTH]
        v = proj[..., 2 * ATTN_WIDTH:3 * ATTN_WIDTH]
        u = proj[..., 3 * ATTN_WIDTH:]

        q = apply_rope(q.reshape(B, S, N_ATTN_HEADS, HEAD_DIM).astype(f32), cos, sin)
        k = apply_rope(k.reshape(B, S, N_ATTN_HEADS, HEAD_DIM).astype(f32), cos, sin)
        v = v.reshape(B, S, N_ATTN_HEADS, HEAD_DIM).astype(f32)
        outs, lses = [], []
        for g, (win, dil) in enumerate(DILATED_GROUPS):
            sl = slice(g * HEADS_PER_GROUP, (g + 1) * HEADS_PER_GROUP)
            o_g, lse_g = dilated_window_attention(q[:, :, sl], k[:, :, sl], v[:, :, sl], win, dil)
            outs.append(o_g)
            lses.append(lse_g)
        alpha = jax.nn.softmax(jnp.stack(lses, axis=0), axis=0)
        o_attn = jnp.sum(alpha[..., None] * jnp.stack(outs, axis=0), axis=0)
        o_attn = o_attn.reshape(B, S, ATTN_OUT_WIDTH).astype(x.dtype)

        uc = dwconv3(u, hy_conv_w[l], hy_conv_b[l])
        hv = uc[..., :HYENA_WIDTH]
        gates_h = (uc[..., HYENA_WIDTH:2 * HYENA_WIDTH], uc[..., 2 * HYENA_WIDTH:])
        kf = hyena_filter_spectrum(feats, t, f_w1[l], f_b1[l], f_freq[l], f_w2[l], f_b2[l], f_w3[l])
        z = hv.astype(f32)
        for o in range(HYENA_ORDER):
            z = gates_h[o].astype(f32) * bidirectional_long_conv(z, kf[:, o], hy_skip[l, o])
        o_hy = z.astype(x.dtype)

        gate = jax.nn.sigmoid(h @ w_gate[l] + b_gate[l])
        mixed = gate[..., :D_MODEL] * (o_attn @ w_proj_attn[l]) + gate[..., D_MODEL:] * (o_hy @ w_proj_hyena[l])
        x = x + mixed @ w_out[l]

        h = rmsnorm(x, ffn_norm[l])
        up = h @ w_up[l]
        a = dwconv3(up[..., :D_FF], ffn_conv_w[l], ffn_conv_b[l])
        x = x + (jax.nn.gelu(a) * up[..., D_FF:]) @ w_down[l]
    return rmsnorm(x, final_norm)
```

```python
import math
from contextlib import ExitStack
import numpy as np
import ml_dtypes
import concourse.bass as bass
import concourse.mybir as mybir
from concourse.bass_utils import run_bass_kernel_spmd

F32 = mybir.dt.float32
BF16 = mybir.dt.bfloat16
I32 = mybir.dt.int32
AF = mybir.ActivationFunctionType
ALU = mybir.AluOpType
NPBF = ml_dtypes.bfloat16

D = 2048
S = 2048
DEPTH = 2
AW = 1536
HWD = 1024
DFF = 5632
INW = 7680
EPS = 1e-6
NCORES = 8


class Res:
    __slots__ = ("lw", "rd")

    def __init__(self):
        self.lw = None
        self.rd = []


class Op:
    __slots__ = ("eng", "fn", "deps", "dma", "sem", "val", "prewait")


ENGS = ("sp", "act", "pool", "dve", "pe")
NDS = 8


class Phase:
    def __init__(self, nc, name):
        self.nc = nc
        self.name = name
        self.ops = []

    def op(self, eng, fn, reads=(), writes=(), dma=False):
        o = Op()
        o.eng = eng
        o.fn = fn
        o.dma = dma
        deps = []
        for r in reads:
            if r.lw is not None:
                deps.append(r.lw)
        for w in writes:
            if w.lw is not None:
                deps.append(w.lw)
            deps.extend(w.rd)
        for r in reads:
            r.rd.append(o)
        for w in writes:
            w.lw = o
            w.rd = []
        o.deps = deps
        self.ops.append(o)
        return o

    def dma(self, eng, out, in_, reads=(), writes=()):
        return self.op(eng, lambda e: e.dma_start(out=out, in_=in_), reads, writes, dma=True)

    def act(self, out, in_, func, reads=(), writes=(), **kw):
        return self.op("act", lambda e: e.activation(out=out, in_=in_, func=func, **kw), reads, writes)

    def mm(self, ps, pairs, reads=(), writes=()):
        pairs = list(pairs)

        def fn(e):
            n = len(pairs)
            ins = None
            for i, (l, r) in enumerate(pairs):
                ins = e.matmul(ps, l, r, start=(i == 0), stop=(i == n - 1))
            return ins
        return self.op("pe", fn, reads, writes)

    def mm_multi(self, groups, reads=(), writes=()):
        groups = [(ps, list(pairs)) for ps, pairs in groups]

        def fn(e):
            ins = None
            for ps, pairs in groups:
                n = len(pairs)
                for i, (l, r) in enumerate(pairs):
                    ins = e.matmul(ps, l, r, start=(i == 0), stop=(i == n - 1))
            return ins
        return self.op("pe", fn, reads, writes)

    def tt(self, eng, out, in0, in1, op, reads=(), writes=()):
        return self.op(eng, lambda e: e.tensor_tensor(out, in0, in1, op), reads, writes)

    def ts(self, eng, out, in0, s1, s2, op0, op1=None, reads=(), writes=()):
        if op1 is None:
            return self.op(eng, lambda e: e.tensor_scalar(out, in0, s1, None, op0), reads, writes)
        return self.op(eng, lambda e: e.tensor_scalar(out, in0, s1, s2, op0, op1), reads, writes)

    def stt(self, eng, out, in0, scalar, in1, op0, op1, reads=(), writes=()):
        return self.op(eng, lambda e: e.scalar_tensor_tensor(out, in0, scalar, in1, op0, op1), reads, writes)

    def copy(self, eng, out, in_, reads=(), writes=()):
        if eng == "act":
            return self.act(out, in_, AF.Copy, reads, writes)
        return self.op(eng, lambda e: e.tensor_copy(out, in_), reads, writes)

    def memset(self, eng, ap, val, writes=()):
        return self.op(eng, lambda e: e.memset(ap, val), (), writes)

    def run(self):
        nc = self.nc
        snap = nc.snapshot_sems()
        csem = {e: nc.alloc_semaphore(f"{self.name}_c_{e}") for e in ENGS}
        dq = [e for e in ENGS if any(o.dma and o.eng == e for o in self.ops)]
        dsem = {e: [nc.alloc_semaphore(f"{self.name}_d_{e}{i}") for i in range(NDS)] for e in dq}
        skey = {}
        for e in ENGS:
            skey[id(csem[e])] = ("c", e)
        cnt = {e: 0 for e in ENGS}
        dcnt = {e: 0 for e in ENGS}
        for o in self.ops:
            if o.dma:
                i = dcnt[o.eng]
                dcnt[o.eng] += 1
                o.sem = ("d", o.eng, i % NDS)
                o.val = 16 * (i // NDS + 1)
                o.prewait = (o.sem, 16 * (i // NDS)) if i >= NDS else None
            else:
                cnt[o.eng] += 1
                o.sem = ("c", o.eng)
                o.val = cnt[o.eng]
                o.prewait = None

        def handle(key):
            return csem[key[1]] if key[0] == "c" else dsem[key[1]][key[2]]

        ops = self.ops

        def emit(en, e):
            waited = {}
            for o in ops:
                if o.eng != en:
                    continue
                need = {}
                for d in o.deps:
                    if d.eng == "pe" and en == "pe" and (not d.dma) and (not o.dma):
                        continue
                    if need.get(d.sem, 0) < d.val:
                        need[d.sem] = d.val
                if o.prewait is not None:
                    k, v = o.prewait
                    if need.get(k, 0) < v:
                        need[k] = v
                for k, v in need.items():
                    if waited.get(k, 0) < v:
                        e.wait_ge(handle(k), v)
                        waited[k] = v
                ins = o.fn(e)
                ins.then_inc(handle(o.sem), 16 if o.dma else 1)
            if en == "sp":
                for x in ENGS:
                    if cnt[x] > 0 and not (x == "sp"):
                        e.wait_ge(csem[x], cnt[x])
                for x in dq:
                    for j in range(NDS):
                        n = (dcnt[x] - j + NDS - 1) // NDS
                        if n > 0:
                            e.wait_ge(dsem[x][j], 16 * n)

        with nc.Block() as block:
            @block.sync
            def _(e):
                emit("sp", e)

            @block.scalar
            def _(e):
                emit("act", e)

            @block.gpsimd
            def _(e):
                emit("pool", e)

            @block.vector
            def _(e):
                emit("dve", e)

            @block.tensor
            def _(e):
                emit("pe", e)
        nc.clear_and_free_semaphores(nc.allocated_since(snap))
        nc.all_engine_barrier()
        self.ops = []


class Rot:
    def __init__(self, tiles):
        self.tiles = tiles
        self.res = [Res() for _ in tiles]
        self.i = -1

    def next(self):
        self.i = (self.i + 1) % len(self.tiles)
        return self.tiles[self.i], self.res[self.i]


def sb(es, nc, name, shape, dt):
    return es.enter_context(nc.sbuf_tensor(name, list(shape), dt))


def psb(es, nc, name, shape, dt=F32):
    return es.enter_context(nc.psum_tensor(name, list(shape), dt))


VC_AN = 0
VC_FN = 16
VC_HCW = 32
VC_HCB = 104
VC_FCW = 128
VC_FCB = 260
VC_BG = 304
VC_FIN = 336
VC_FB1 = 352
VC_FQ0 = 353
VC_FQ1 = 354
VC_FB2 = 355
NV = 356


def _cols(v, n):
    return np.ascontiguousarray(v.reshape(n, 128).T)


def make_vecs(inp, l):
    v = np.zeros((128, NV), np.float32)
    v[:, VC_AN:VC_AN + 16] = _cols(inp["attn_norm"][l], 16)
    v[:, VC_FN:VC_FN + 16] = _cols(inp["ffn_norm"][l], 16)
    for t in range(3):
        v[:, VC_HCW + 24 * t:VC_HCW + 24 * (t + 1)] = _cols(inp["hy_conv_w"][l, t], 24)
        v[:, VC_FCW + 44 * t:VC_FCW + 44 * (t + 1)] = _cols(inp["ffn_conv_w"][l, t], 44)
    v[:, VC_HCB:VC_HCB + 24] = _cols(inp["hy_conv_b"][l], 24)
    v[:, VC_FCB:VC_FCB + 44] = _cols(inp["ffn_conv_b"][l], 44)
    v[:, VC_BG:VC_BG + 32] = _cols(inp["b_gate"][l], 32)
    v[:, VC_FIN:VC_FIN + 16] = _cols(inp["final_norm"], 16)
    v[:64, VC_FB1] = inp["f_b1"][l]
    v[:64, VC_FQ0] = inp["f_freq"][l, 0]
    v[:64, VC_FQ1] = inp["f_freq"][l, 1]
    v[:64, VC_FB2] = inp["f_b2"][l]
    return v


_CONSTS = None


def make_consts():
    global _CONSTS
    if _CONSTS is not None:
        return _CONSTS
    c = {}
    pos = np.arange(S, dtype=np.float32)
    inv = (1.0 / (10000.0 ** (np.arange(0, 64, 2, dtype=np.float32) / 64))).astype(np.float32)
    ang = pos[:, None] * inv[None, :]
    ang = np.concatenate([ang, ang], axis=-1)
    cosT = np.cos(ang).T.astype(np.float32)
    sinT = np.sin(ang).T.astype(np.float32)
    c["ropec"] = np.ascontiguousarray(np.concatenate([cosT, cosT], 0))
    c["ropes"] = np.ascontiguousarray(np.concatenate([sinT, sinT], 0))
    R = np.zeros((128, 128), np.float32)
    for m in range(128):
        if (m % 64) < 32:
            R[m + 32, m] = -1.0
        else:
            R[m - 32, m] = 1.0
    c["rotm"] = R.astype(NPBF)
    c["ident"] = np.eye(128, dtype=np.float32).astype(NPBF)
    c["ones"] = np.ones((128, 128), np.float32).astype(NPBF)
    i = np.arange(128)[:, None]
    j = np.arange(256)[None, :]
    c["mask"] = (((j - i) >= 0) & ((j - i) <= 128)).astype(np.float32).astype(NPBF)
    N = 2 * S
    tab_c = np.cos(2 * np.pi * np.arange(N) / N)
    tab_s = np.sin(2 * np.pi * np.arange(N) / N)
    n = np.arange(N, dtype=np.int64)[:, None]
    o = np.arange(N, dtype=np.int64)[None, :]
    f = np.where(o < S, o, o - S)
    idx = (f * n) % N
    FK = np.where(o < S, tab_c[idx], -tab_s[idx])
    FK[:, S] = np.where((np.arange(N) % 2) == 0, 1.0, -1.0)
    c["fkl"] = np.ascontiguousarray(FK.reshape(32, 128, 32, 128).transpose(2, 1, 0, 3)).astype(NPBF)
    del FK
    oo = np.arange(N, dtype=np.int64)[:, None]
    t = np.arange(S, dtype=np.int64)[None, :]
    ff = np.where(oo < S, oo, oo - S)
    idx = (ff * t) % N
    IW = np.where(oo < S, 2.0 * tab_c[idx], -2.0 * tab_s[idx]) / N
    IW[0, :] = 1.0 / N
    IW[S, :] = np.where((np.arange(S) % 2) == 0, 1.0, -1.0) / N
    c["iwa"] = np.ascontiguousarray(IW.reshape(32, 128, 16, 128).transpose(2, 1, 0, 3)).astype(NPBF)
    c["iwb"] = np.ascontiguousarray(IW.reshape(32, 128, 4, 512).transpose(2, 1, 0, 3)).astype(NPBF)
    del IW
    tt = np.linspace(0.0, 1.0, S, dtype=np.float32)[:, None]
    bands = 16
    w = (2.0 * math.pi * np.arange(S, dtype=np.float32)[:, None] / S).astype(np.float32)
    fr = np.linspace(1e-4, bands - 1, bands, dtype=np.float32)[None, :]
    feats = np.concatenate([tt, np.cos(fr * w), -np.sin(fr * w)], axis=-1).astype(np.float32)
    rev = np.zeros_like(feats)
    rev[1:] = feats[:0:-1]
    rev[0] = feats[0]
    c["feats"] = np.ascontiguousarray(np.concatenate([feats.T, rev.T], axis=1))
    max_decay = math.log(1e-2) / 0.3
    min_decay = math.log(1e-2) / 1.5
    deltas = np.linspace(min_decay, max_decay, HWD, dtype=np.float32)
    decay = np.exp(-tt * np.abs(deltas)[None, :]).astype(np.float32)
    drev = np.zeros_like(decay)
    drev[1:] = decay[:0:-1]
    c["decay"] = np.ascontiguousarray(np.concatenate([decay, drev], 0))
    _CONSTS = c
    return c


def phase_norm(nc, name, x_src, vecs_l, gcol, h_dst, final=False):
    ph = Phase(nc, name)
    TW = 256
    NT = S // TW
    with ExitStack() as es:
        vec = sb(es, nc, name + "_vec", [128, NV], F32)
        ones = sb(es, nc, name + "_ones", [128, 128], BF16)
        xt = [sb(es, nc, f"{name}_x{i}", [128, 16, TW], F32) for i in range(4)]
        sq = [sb(es, nc, f"{name}_sq{i}", [128, 16, TW], BF16) for i in range(2)]
        ht = [sb(es, nc, f"{name}_h{i}", [128, 16, TW], F32 if final else BF16) for i in range(3)]
        rms = [sb(es, nc, f"{name}_rms{i}", [128, TW], F32) for i in range(2)]
        ps = [psb(es, nc, f"{name}_ps{i}", [128, 512]) for i in range(2)]
        r_vec, r_ones = Res(), Res()
        X, SQ, H, RMS, PS = Rot(xt), Rot(sq), Rot(ht), Rot(rms), Rot(ps)
        ph.dma("sp", vec[:], vecs_l, writes=[r_vec])
        ph.memset("pool", ones[:], 1.0, writes=[r_ones])
        xs = x_src.rearrange("(kc p) t -> p kc t", p=128)
        hd = h_dst.rearrange("(kc p) t -> p kc t", p=128)
        for tt in range(NT):
            x, rx = X.next()
            ph.dma("sp", x[:], xs[:, :, tt * TW:(tt + 1) * TW], writes=[rx])
            s, rs = SQ.next()
            ph.act(s[:], x[:], AF.Square, reads=[rx], writes=[rs])
            p, rp = PS.next()
            ph.mm(p[:, 0:TW], [(ones[:], s[:, k, :]) for k in range(16)], reads=[r_ones, rs], writes=[rp])
            r, rr = RMS.next()
            ph.act(r[:], p[:, 0:TW], AF.Sqrt, reads=[rp], writes=[rr], scale=1.0 / D, bias=EPS)
            ph.op("dve", lambda e, r=r: e.reciprocal(r[:], r[:]), reads=[rr], writes=[rr])
            h, rh = H.next()
            for k in range(16):
                ph.stt("dve", h[:, k, :], x[:, k, :], vec[:, gcol + k:gcol + k + 1], r[:], ALU.mult, ALU.mult,
                       reads=[rx, rr, r_vec], writes=[rh])
            ph.dma("pool", hd[:, :, tt * TW:(tt + 1) * TW], h[:], reads=[rh])
        ph.run()


def load_resident(ph, eng, dst, src_dram, kc, res):
    sv = src_dram.rearrange("(kc p) t -> p kc t", p=128)
    for tt in range(4):
        ph.dma(eng, dst[:, :, tt * 512:(tt + 1) * 512], sv[:, :, tt * 512:(tt + 1) * 512], writes=[res[tt]])


def wview(w_ap, c0, nc_):
    return w_ap.rearrange("(kc p) n -> p kc n", p=128)[:, :, c0:c0 + nc_]


def conv3(ph, ubuf, r_u, vec, r_vec, c0, c1, c2, cb, tmp, r_tmp, out, r_out):
    ph.ts("dve", tmp[:], ubuf[:, 1:2049], vec[:, c1:c1 + 1], vec[:, cb:cb + 1], ALU.mult, ALU.add,
          reads=[r_u, r_vec], writes=[r_tmp])
    ph.stt("dve", tmp[:], ubuf[:, 0:2048], vec[:, c0:c0 + 1], tmp[:], ALU.mult, ALU.add,
           reads=[r_u, r_vec, r_tmp], writes=[r_tmp])
    ph.stt("dve", out, ubuf[:, 2:2050], vec[:, c2:c2 + 1], tmp[:], ALU.mult, ALU.add,
           reads=[r_u, r_vec, r_tmp], writes=[r_out])


def phase_inproj(nc, name, hT, w_in, vecs_l, cst, qT, kT, vaug, z0, g0, g1T):
    ph = Phase(nc, name)
    with ExitStack() as es:
        hres = sb(es, nc, name + "_h", [128, 16, 2048], BF16)
        vec = sb(es, nc, name + "_vec", [128, NV], F32)
        cosT = sb(es, nc, name + "_cos", [128, 2048], F32)
        sinT = sb(es, nc, name + "_sin", [128, 2048], F32)
        rotm = sb(es, nc, name + "_rot", [128, 128], BF16)
        ident = sb(es, nc, name + "_id", [128, 128], BF16)
        wt = [sb(es, nc, f"{name}_w{i}", [128, 16, 512], BF16) for i in range(3)]
        qs = [sb(es, nc, f"{name}_qs{i}", [128, 512], BF16) for i in range(4)]
        t1 = [sb(es, nc, f"{name}_t1{i}", [128, 512], F32) for i in range(2)]
        t2 = [sb(es, nc, f"{name}_t2{i}", [128, 512], F32) for i in range(2)]
        qo = [sb(es, nc, f"{name}_qo{i}", [128, 2048], BF16) for i in range(3)]
        vo = [sb(es, nc, f"{name}_vo{i}", [128, 8, 128], BF16) for i in range(2)]
        ub = [sb(es, nc, f"{name}_ub{i}", [128, 2050], F32) for i in range(2)]
        tmp = [sb(es, nc, f"{name}_tmp{i}", [128, 2048], F32) for i in range(1)]
        tm = [sb(es, nc, f"{name}_tm{i}", [128, 16, 128], BF16) for i in range(2)]
        ps = [psb(es, nc, f"{name}_ps{i}", [128, 512]) for i in range(4)]
        pr = [psb(es, nc, f"{name}_pr{i}", [128, 512]) for i in range(2)]
        pt = [psb(es, nc, f"{name}_pt{i}", [128, 512], BF16) for i in range(2)]
        r_vec, r_c, r_rot, r_id = Res(), Res(), Res(), Res()
        r_h = [Res() for _ in range(4)]
        W, QS, T1, T2, QO, VO, UB, TMP, TM, PS, PR, PT = (Rot(wt), Rot(qs), Rot(t1), Rot(t2), Rot(qo), Rot(vo),
                                                          Rot(ub), Rot(tmp), Rot(tm), Rot(ps), Rot(pr), Rot(pt))
        ph.dma("sp", vec[:], vecs_l, writes=[r_vec])
        ph.dma("sp", cosT[:], cst["ropec"], writes=[r_c])
        ph.dma("sp", sinT[:], cst["ropes"], writes=[r_c])
        ph.dma("sp", rotm[:], cst["rotm"], writes=[r_rot])
        ph.dma("sp", ident[:], cst["ident"], writes=[r_id])
        load_resident(ph, "sp", hres, hT, 16, r_h)
        for v_, rv in zip(VO.tiles, VO.res):
            ph.memset("pool", v_[:], 1.0, writes=[rv])
        for u_, ru in zip(UB.tiles, UB.res):
            ph.memset("pool", u_[:], 0.0, writes=[ru])

        wcols = ([blk * 512 for blk in range(6)] + [2 * AW + g * 512 for g in range(3)]
                 + [3 * AW + blk * 512 for blk in range(6)])
        wq = []

        def wload(i):
            w, rw = W.next()
            ph.dma("pool", w[:], wview(w_in, wcols[i], 512), writes=[rw])
            wq.append((w, rw))

        def getw(i):
            while len(wq) < min(i + 3, len(wcols)):
                wload(len(wq))
            return wq[i]

        qk_pend = []

        def qk_stage_b(item):
            q, rq, o, ro, tsl, tt, m = item
            p2, rp2 = PR.next()
            ph.mm(p2[:], [(rotm[:], q[:])], reads=[r_rot, rq], writes=[rp2])
            a, ra = T1.next()
            ph.tt("dve", a[:], q[:], cosT[:, tsl], ALU.mult, reads=[rq, r_c], writes=[ra])
            b, rb = T2.next()
            ph.tt("dve", b[:], p2[:], sinT[:, tsl], ALU.mult, reads=[rp2, r_c], writes=[rb])
            ph.tt("pool", o[:, tsl], a[:], b[:], ALU.add, reads=[ra, rb], writes=[ro])
            if tt == 3:
                dst = qT if m < 12 else kT
                mm_ = m % 12
                ph.dma("sp", dst[mm_ * 128:(mm_ + 1) * 128, :], o[:], reads=[ro])

        for blk in range(6):
            w, rw = getw(blk)
            for mi in range(4):
                m = blk * 4 + mi
                o, ro = QO.next()
                for tt in range(4):
                    tsl = slice(tt * 512, (tt + 1) * 512)
                    p, rp = PS.next()
                    ph.mm(p[:], [(w[:, k, mi * 128:(mi + 1) * 128], hres[:, k, tsl]) for k in range(16)],
                          reads=[rw, r_h[tt]], writes=[rp])
                    q, rq = QS.next()
                    ph.copy("act", q[:], p[:], reads=[rp], writes=[rq])
                    qk_pend.append((q, rq, o, ro, tsl, tt, m))
                    if len(qk_pend) > 1:
                        qk_stage_b(qk_pend.pop(0))
        while qk_pend:
            qk_stage_b(qk_pend.pop(0))

        for g in range(3):
            w, rw = getw(6 + g)
            for tk in range(16):
                p, rp = PS.next()
                ph.mm(p[:], [(hres[:, k, tk * 128:(tk + 1) * 128], w[:, k, :]) for k in range(16)],
                      reads=[rw, r_h[tk // 4]], writes=[rp])
                v_, rv = VO.next()
                ph.copy("act", v_[:, :, 0:64], p[:].rearrange("p (h d) -> p h d", d=64), reads=[rp], writes=[rv])
                ph.dma("sp", vaug[tk * 128:(tk + 1) * 128, g * 8:(g + 1) * 8, :], v_[:], reads=[rv])

        u_pend = []

        def u_stage_b(item):
            uc, o, ro = item
            if uc >= 16:
                ph.dma("sp", g1T[(uc - 16) * 128:(uc - 15) * 128, :], o[:], reads=[ro])
                return
            t_, rt = TM.next()
            for q4 in range(4):
                pp, rpp = PT.next()
                for j in range(4):
                    tk = q4 * 4 + j
                    ph.op("pe", lambda e, pp=pp, j=j, tk=tk, o=o: e.transpose(
                        pp[:, j * 128:(j + 1) * 128], o[:, tk * 128:(tk + 1) * 128], ident[:]),
                        reads=[ro, r_id], writes=[rpp])
                ph.copy("act", t_[:, q4 * 4:(q4 + 1) * 4, :], pp[:].rearrange("p (a c) -> p a c", c=128),
                        reads=[rpp], writes=[rt])
            dst = z0 if uc < 8 else g0
            cc = uc % 8
            ph.dma("sp", dst.rearrange("(tk p) c -> p tk c", p=128)[:, :, cc * 128:(cc + 1) * 128], t_[:],
                   reads=[rt])

        for blk in range(6):
            w, rw = getw(9 + blk)
            for mi in range(4):
                uc = blk * 4 + mi
                u_, ru = UB.next()
                for tt in range(4):
                    tsl = slice(tt * 512, (tt + 1) * 512)
                    p, rp = PS.next()
                    ph.mm(p[:], [(w[:, k, mi * 128:(mi + 1) * 128], hres[:, k, tsl]) for k in range(16)],
                          reads=[rw, r_h[tt]], writes=[rp])
                    ph.copy("act", u_[:, 1 + tt * 512:1 + (tt + 1) * 512], p[:], reads=[rp], writes=[ru])
                o, ro = QO.next()
                tp, rtp = TMP.next()
                conv3(ph, u_, ru, vec, r_vec, VC_HCW + uc, VC_HCW + 24 + uc, VC_HCW + 48 + uc, VC_HCB + uc,
                      tp, rtp, o[:], ro)
                u_pend.append((uc, o, ro))
                if len(u_pend) > 1:
                    u_stage_b(u_pend.pop(0))
        while u_pend:
            u_stage_b(u_pend.pop(0))
        ph.run()


def ssl(start, n, step):
    return slice(start, start + (n - 1) * step + 1, step)


def phase_attn(nc, name, qT, kT, vaug, cst, oT):
    ph = Phase(nc, name)
    with ExitStack() as es:
        acc = sb(es, nc, name + "_acc", [128, 8, 2048], F32)
        qg = [sb(es, nc, f"{name}_qg{i}", [64, 8, 2048], BF16) for i in range(1)]
        kg = [sb(es, nc, f"{name}_kg{i}", [64, 8, 2048], BF16) for i in range(1)]
        mask = sb(es, nc, name + "_mask", [128, 2, 256], BF16)
        vt = [sb(es, nc, f"{name}_vt{i}", [128, 8, 128], BF16) for i in range(4)]
        pt = [sb(es, nc, f"{name}_pt{i}", [128, 2, 256], BF16) for i in range(7)]
        rz = [sb(es, nc, f"{name}_rz{i}", [128, 2048], F32) for i in range(2)]
        ot = [sb(es, nc, f"{name}_ot{i}", [128, 2048], BF16) for i in range(2)]
        pss = [psb(es, nc, f"{name}_pss{i}", [128, 512]) for i in range(4)]
        pso = [psb(es, nc, f"{name}_pso{i}", [128, 512]) for i in range(4)]
        r_mask = Res()
        r_acc = [Res() for _ in range(4)]
        QG, KG, VT, PT, RZ, OT, PSS, PSO = Rot(qg), Rot(kg), Rot(vt), Rot(pt), Rot(rz), Rot(ot), Rot(pss), Rot(pso)
        for hh in range(2):
            ph.dma("sp", mask[:, hh, :], cst["mask"], writes=[r_mask])
        for j in range(4):
            ph.memset("pool", acc[:, 2 * j:2 * j + 2, :], 0.0, writes=[r_acc[j]])
        for g, r in enumerate((1, 4, 16)):
            T = S // r
            nkb = T // 128
            q_, rq = QG.next()
            k_, rk = KG.next()
            qv = qT[g * 512:(g + 1) * 512, :].rearrange("(c p) t -> p c t", p=64)
            kv = kT[g * 512:(g + 1) * 512, :].rearrange("(c p) t -> p c t", p=64)
            for half in range(2):
                hs = slice(half * 1024, (half + 1) * 1024)
                ph.dma("sp", q_[:, :, hs], qv[:, :, hs], writes=[rq])
                ph.dma("sp", k_[:, :, hs], kv[:, :, hs], writes=[rk])
            pend = []

            def stage_b(item):
                p, rp, v_, rv, j, nq, qsl = item
                pO, rpO = PSO.next()
                ph.mm_multi([(pO[:, hh * 256:hh * 256 + nq], [(v_[:, 2 * j + hh, :], p[:, hh, 0:nq])])
                             for hh in range(2)], reads=[rv, rp], writes=[rpO])
                pOv = pO[:].rearrange("p (h q) -> p h q", q=256)
                ph.tt("dve", acc[:, 2 * j:2 * j + 2, qsl], acc[:, 2 * j:2 * j + 2, qsl], pOv[:, :, 0:nq], ALU.add,
                      reads=[rpO, r_acc[j]], writes=[r_acc[j]])

            for c in range(r):
                for kb in range(nkb):
                    v_, rv = VT.next()
                    ph.dma("sp", v_[:], vaug[ssl(c + r * 128 * kb, 128, r), g * 8:(g + 1) * 8, :],
                           writes=[rv])
                    jlo = 64 if kb == 0 else 0
                    jhi = 192 if kb == nkb - 1 else 256
                    nq = jhi - jlo
                    q0 = 128 * kb - 64 + jlo
                    qsl = ssl(c + r * q0, nq, r)
                    ksl = ssl(c + r * 128 * kb, 128, r)
                    for j in range(4):
                        pS, rpS = PSS.next()
                        ph.mm_multi([(pS[:, hh * 256:hh * 256 + nq],
                                      [(k_[:, 2 * j + hh, ksl], q_[:, 2 * j + hh, qsl])])
                                     for hh in range(2)], reads=[rq, rk], writes=[rpS])
                        p, rp = PT.next()
                        pSv = pS[:].rearrange("p (h q) -> p h q", q=256)
                        ph.act(p[:, :, 0:nq], pSv[:, :, 0:nq], AF.Exp, reads=[rpS], writes=[rp], scale=0.125)
                        ph.tt("pool", p[:, :, 0:nq], p[:, :, 0:nq], mask[:, :, jlo:jhi], ALU.mult,
                              reads=[rp, r_mask], writes=[rp])
                        pend.append((p, rp, v_, rv, j, nq, qsl))
                        if len(pend) > 3:
                            stage_b(pend.pop(0))
            while pend:
                stage_b(pend.pop(0))
        for h in range(8):
            z_, rzr = RZ.next()
            ph.op("dve", lambda e, z_=z_, h=h: e.reciprocal(z_[0:64, :], acc[64:128, h, :]),
                  reads=[r_acc[h // 2]], writes=[rzr])
            o_, ro = OT.next()
            ph.tt("dve", o_[0:64, :], acc[0:64, h, :], z_[0:64, :], ALU.mult, reads=[r_acc[h // 2], rzr], writes=[ro])
            ph.dma("sp", oT[h * 64:(h + 1) * 64, :], o_[0:64, :], reads=[ro])
        ph.run()


def phase_gate(nc, name, hT, oT, ohT, w_gate, w_pa, w_ph, vecs_l, mixT):
    ph = Phase(nc, name)
    with ExitStack() as es:
        hres = sb(es, nc, name + "_h", [128, 16, 2048], BF16)
        ores = sb(es, nc, name + "_o", [128, 4, 2048], BF16)
        yres = sb(es, nc, name + "_y", [128, 8, 2048], BF16)
        vec = sb(es, nc, name + "_vec", [128, NV], F32)
        wga = [sb(es, nc, f"{name}_wga{i}", [128, 16, 256], BF16) for i in range(2)]
        wgh = [sb(es, nc, f"{name}_wgh{i}", [128, 16, 256], BF16) for i in range(2)]
        wpa = [sb(es, nc, f"{name}_wpa{i}", [128, 4, 256], BF16) for i in range(2)]
        wph = [sb(es, nc, f"{name}_wph{i}", [128, 8, 256], BF16) for i in range(2)]
        ga = [sb(es, nc, f"{name}_ga{i}", [128, 512], F32) for i in range(2)]
        gh = [sb(es, nc, f"{name}_gh{i}", [128, 512], F32) for i in range(2)]
        mo = [sb(es, nc, f"{name}_mo{i}", [128, 2048], BF16) for i in range(2)]
        ps = [psb(es, nc, f"{name}_ps{i}", [128, 512]) for i in range(8)]
        r_vec = Res()
        r_h, r_o, r_y = [Res() for _ in range(4)], [Res() for _ in range(4)], [Res() for _ in range(4)]
        WGA, WGH, WPA, WPH, GA, GH, MO, PS = Rot(wga), Rot(wgh), Rot(wpa), Rot(wph), Rot(ga), Rot(gh), Rot(mo), Rot(ps)
        ph.dma("sp", vec[:], vecs_l, writes=[r_vec])
        load_resident(ph, "sp", hres, hT, 16, r_h)
        load_resident(ph, "sp", ores, oT, 4, r_o)
        load_resident(ph, "sp", yres, ohT, 8, r_y)
        wq = []

        def wload(i):
            c0 = i * 256
            a_, ra = WGA.next()
            ph.dma("pool", a_[:], wview(w_gate, c0, 256), writes=[ra])
            b_, rb = WGH.next()
            ph.dma("pool", b_[:], wview(w_gate, D + c0, 256), writes=[rb])
            c_, rc = WPA.next()
            ph.dma("pool", c_[:], wview(w_pa, c0, 256), writes=[rc])
            d_, rd = WPH.next()
            ph.dma("pool", d_[:], wview(w_ph, c0, 256), writes=[rd])
            wq.append((a_, ra, b_, rb, c_, rc, d_, rd))
        wload(0)
        for blk in range(8):
            if blk + 1 < 8:
                wload(blk + 1)
            a_, ra, b_, rb, c_, rc, d_, rd = wq[blk]
            for mi in range(2):
                m = blk * 2 + mi
                msl = slice(mi * 128, (mi + 1) * 128)
                o_, ro = MO.next()
                for tt in range(4):
                    tsl = slice(tt * 512, (tt + 1) * 512)
                    pA, rpA = PS.next()
                    ph.mm(pA[:], [(a_[:, k, msl], hres[:, k, tsl]) for k in range(16)], reads=[ra, r_h[tt]], writes=[rpA])
                    pB, rpB = PS.next()
                    ph.mm(pB[:], [(b_[:, k, msl], hres[:, k, tsl]) for k in range(16)], reads=[rb, r_h[tt]], writes=[rpB])
                    pC, rpC = PS.next()
                    ph.mm(pC[:], [(c_[:, k, msl], ores[:, k, tsl]) for k in range(4)], reads=[rc, r_o[tt]], writes=[rpC])
                    pD, rpD = PS.next()
                    ph.mm(pD[:], [(d_[:, k, msl], yres[:, k, tsl]) for k in range(8)], reads=[rd, r_y[tt]], writes=[rpD])
                    g1_, rg1 = GA.next()
                    ph.act(g1_[:], pA[:], AF.Sigmoid, reads=[rpA, r_vec], writes=[rg1], bias=vec[:, VC_BG + m:VC_BG + m + 1])
                    g2_, rg2 = GH.next()
                    ph.act(g2_[:], pB[:], AF.Sigmoid, reads=[rpB, r_vec], writes=[rg2],
                           bias=vec[:, VC_BG + 16 + m:VC_BG + 16 + m + 1])
                    ph.tt("dve", g1_[:], g1_[:], pC[:], ALU.mult, reads=[rg1, rpC], writes=[rg1])
                    ph.tt("dve", g2_[:], g2_[:], pD[:], ALU.mult, reads=[rg2, rpD], writes=[rg2])
                    ph.tt("pool", o_[:, tsl], g1_[:], g2_[:], ALU.add, reads=[rg1, rg2], writes=[ro])
                ph.dma("sp", mixT[m * 128:(m + 1) * 128, :], o_[:], reads=[ro])
        ph.run()


def phase_proj_res(nc, name, aT, kc, w, x_src, x_dst, wcols, tpp):
    ph = Phase(nc, name)
    with ExitStack() as es:
        na = 4 if tpp == 4 else tpp + 1
        at = [sb(es, nc, f"{name}_a{i}", [128, kc, 512], BF16) for i in range(na)]
        wt = [sb(es, nc, f"{name}_w{i}", [128, kc, wcols], BF16) for i in range(2)]
        xt = [sb(es, nc, f"{name}_x{i}", [128, 512], F32) for i in range(5)]
        ps = [psb(es, nc, f"{name}_ps{i}", [128, 512]) for i in range(4)]
        AT, WT, XT, PS = Rot(at), Rot(wt), Rot(xt), Rot(ps)
        av = aT.rearrange("(kc p) t -> p kc t", p=128)
        nm = wcols // 128
        npass = 4 // tpp
        aq = []
        xq = []
        items = [(pss_ * tpp + ti, blk, mi) for pss_ in range(npass) for blk in range(D // wcols)
                 for mi in range(nm) for ti in range(tpp)]

        def aload(tt):
            a_, ra = AT.next()
            ph.dma("sp", a_[:], av[:, :, tt * 512:(tt + 1) * 512], writes=[ra])
            aq.append((a_, ra))

        def xload(i):
            tt, blk, mi = items[i]
            m = blk * nm + mi
            x_, rx = XT.next()
            ph.dma("sp", x_[:], x_src[m * 128:(m + 1) * 128, tt * 512:(tt + 1) * 512], writes=[rx])
            xq.append((x_, rx))
        for tt in range(min(na, 4)):
            aload(tt)
        xload(0)
        xload(1)
        it = 0
        for pss_ in range(npass):
            while len(aq) < min(4, (pss_ + 1) * tpp):
                aload(len(aq))
            for blk in range(D // wcols):
                w_, rw = WT.next()
                ph.dma("pool", w_[:], wview(w, blk * wcols, wcols), writes=[rw])
                if blk == 0 and pss_ > 0:
                    while len(aq) < min(4, (pss_ + 1) * tpp + 1):
                        aload(len(aq))
                for mi in range(nm):
                    m = blk * nm + mi
                    for ti in range(tpp):
                        tt = pss_ * tpp + ti
                        tsl = slice(tt * 512, (tt + 1) * 512)
                        a_, ra = aq[tt]
                        if it + 2 < len(items):
                            xload(it + 2)
                        x_, rx = xq[it]
                        it += 1
                        p, rp = PS.next()
                        ph.mm(p[:], [(w_[:, k, mi * 128:(mi + 1) * 128], a_[:, k, :]) for k in range(kc)],
                              reads=[rw, ra], writes=[rp])
                        ph.tt("dve", x_[:], x_[:], p[:], ALU.add, reads=[rx, rp], writes=[rx])
                        ph.dma("sp", x_dst[m * 128:(m + 1) * 128, tsl], x_[:], reads=[rx])
        ph.run()


GELU_C = 1.5957691216057308


def phase_ffn_up(nc, name, hT, w_up, vecs_l, actT):
    ph = Phase(nc, name)
    with ExitStack() as es:
        hres = sb(es, nc, name + "_h", [128, 16, 2048], BF16)
        vec = sb(es, nc, name + "_vec", [128, NV], F32)
        wa = [sb(es, nc, f"{name}_wa{i}", [128, 16, 256], BF16) for i in range(3)]
        wg = [sb(es, nc, f"{name}_wg{i}", [128, 16, 256], BF16) for i in range(3)]
        ab = [sb(es, nc, f"{name}_ab{i}", [128, 2050], F32) for i in range(2)]
        gb = [sb(es, nc, f"{name}_gb{i}", [128, 2048], F32) for i in range(2)]
        tmp = [sb(es, nc, f"{name}_tmp{i}", [128, 2048], F32) for i in range(1)]
        av = [sb(es, nc, f"{name}_av{i}", [128, 2048], F32) for i in range(2)]
        sv = [sb(es, nc, f"{name}_sv{i}", [128, 2048], F32) for i in range(2)]
        ob = [sb(es, nc, f"{name}_ob{i}", [128, 2048], BF16) for i in range(2)]
        ps = [psb(es, nc, f"{name}_ps{i}", [128, 512]) for i in range(8)]
        r_vec = Res()
        r_h = [Res() for _ in range(4)]
        WA, WG, AB, GB, TMP, AV, SV, OB, PS = (Rot(wa), Rot(wg), Rot(ab), Rot(gb), Rot(tmp), Rot(av), Rot(sv),
                                                Rot(ob), Rot(ps))
        ph.dma("sp", vec[:], vecs_l, writes=[r_vec])
        load_resident(ph, "sp", hres, hT, 16, r_h)
        for u_, ru in zip(AB.tiles, AB.res):
            ph.memset("pool", u_[:], 0.0, writes=[ru])
        wq = []

        def wload(i):
            a_, ra = WA.next()
            ph.dma("pool", a_[:], wview(w_up, i * 256, 256), writes=[ra])
            g_, rg = WG.next()
            ph.dma("pool", g_[:], wview(w_up, DFF + i * 256, 256), writes=[rg])
            wq.append((a_, ra, g_, rg))
        wload(0)
        wload(1)
        for blk in range(22):
            if blk + 2 < 22:
                wload(blk + 2)
            a_, ra, g_, rg = wq[blk]
            for mi in range(2):
                j = blk * 2 + mi
                msl = slice(mi * 128, (mi + 1) * 128)
                u_, ru = AB.next()
                gg, rgg = GB.next()
                for tt in range(4):
                    tsl = slice(tt * 512, (tt + 1) * 512)
                    p, rp = PS.next()
                    ph.mm(p[:], [(a_[:, k, msl], hres[:, k, tsl]) for k in range(16)], reads=[ra, r_h[tt]], writes=[rp])
                    ph.copy("act", u_[:, 1 + tt * 512:1 + (tt + 1) * 512], p[:], reads=[rp], writes=[ru])
                    p2, rp2 = PS.next()
                    ph.mm(p2[:], [(g_[:, k, msl], hres[:, k, tsl]) for k in range(16)], reads=[rg, r_h[tt]], writes=[rp2])
                    ph.copy("act", gg[:, tsl], p2[:], reads=[rp2], writes=[rgg])
                a2, ra2 = AV.next()
                tp, rtp = TMP.next()
                conv3(ph, u_, ru, vec, r_vec, VC_FCW + j, VC_FCW + 44 + j, VC_FCW + 88 + j, VC_FCB + j, tp, rtp, a2[:], ra2)
                s_, rs = SV.next()
                ph.act(s_[:], a2[:], AF.Square, reads=[ra2], writes=[rs])
                ph.ts("pool", s_[:], s_[:], 0.044715, 1.0, ALU.mult, ALU.add, reads=[rs], writes=[rs])
                ph.tt("pool", s_[:], s_[:], a2[:], ALU.mult, reads=[rs, ra2], writes=[rs])
                ph.act(s_[:], s_[:], AF.Sigmoid, reads=[rs], writes=[rs], scale=GELU_C)
                ph.tt("pool", s_[:], s_[:], a2[:], ALU.mult, reads=[rs, ra2], writes=[rs])
                o_, ro = OB.next()
                ph.tt("dve", o_[:], s_[:], gg[:], ALU.mult, reads=[rs, rgg], writes=[ro])
                ph.dma("sp", actT[j * 128:(j + 1) * 128, :], o_[:], reads=[ro])
        ph.run()


TWO_PI = 2.0 * math.pi
PI_LO = 3.1415925


def phase_filt_gen(nc, name, f_w1, f_w2, f_w3, vecs_l, cst, ktd, rsd):
    ph = Phase(nc, name)
    with ExitStack() as es:
        vec = sb(es, nc, name + "_vec", [128, NV], F32)
        feats = sb(es, nc, name + "_ft", [64, 4096], F32)
        w1 = sb(es, nc, name + "_w1", [64, 64], F32)
        w2 = sb(es, nc, name + "_w2", [64, 64], F32)
        w3 = sb(es, nc, name + "_w3", [64, 4096], F32)
        h1 = sb(es, nc, name + "_h1", [64, 4096], F32)
        h2 = sb(es, nc, name + "_h2", [64, 4096], F32)
        fb = sb(es, nc, name + "_fb", [64, 2], F32)
        ones = sb(es, nc, name + "_ones", [128, 128], BF16)
        arg = [sb(es, nc, f"{name}_arg{i}", [64, 512], F32) for i in range(2)]
        ai = [sb(es, nc, f"{name}_ai{i}", [64, 512], I32) for i in range(2)]
        af = [sb(es, nc, f"{name}_af{i}", [64, 512], F32) for i in range(2)]
        dec = [sb(es, nc, f"{name}_dec{i}", [128, 1024], F32) for i in range(2)]
        kt = [sb(es, nc, f"{name}_kt{i}", [128, 2048], BF16) for i in range(2)]
        ka = [sb(es, nc, f"{name}_ka{i}", [128, 2048], BF16) for i in range(2)]
        rs = sb(es, nc, name + "_rs", [128, 2048], F32)
        pS = [psb(es, nc, f"{name}_pS{i}", [128, 512]) for i in range(4)]
        pw = [psb(es, nc, f"{name}_pw{i}", [128, 512]) for i in range(4)]
        r_vec, r_ft, r_w1, r_w2, r_w3, r_h1, r_h2, r_fb, r_ones, r_S, r_rs = (Res() for _ in range(11))
        ARG, AI, AFR, DEC, KT, KA, PW = Rot(arg), Rot(ai), Rot(af), Rot(dec), Rot(kt), Rot(ka), Rot(pw)
        ph.dma("sp", vec[:], vecs_l, writes=[r_vec])
        ph.dma("sp", feats[0:33, :], cst["feats"], writes=[r_ft])
        ph.dma("sp", w1[0:33, :], f_w1, writes=[r_w1])
        ph.dma("sp", w2[:], f_w2, writes=[r_w2])
        ph.dma("sp", w3[:], f_w3, writes=[r_w3])
        ph.memset("pool", ones[:], 1.0, writes=[r_ones])
        ph.tt("dve", fb[:, 0:1], vec[0:64, VC_FQ0:VC_FQ0 + 1], vec[0:64, VC_FB1:VC_FB1 + 1], ALU.mult,
              reads=[r_vec], writes=[r_fb])
        ph.tt("dve", fb[:, 1:2], vec[0:64, VC_FQ1:VC_FQ1 + 1], vec[0:64, VC_FB2:VC_FB2 + 1], ALU.mult,
              reads=[r_vec], writes=[r_fb])

        def sin_stage(p, rp, fqc, fbc, out, r_out):
            a_, ra = ARG.next()
            ph.ts("dve", a_[:], p[0:64, :], vec[0:64, fqc:fqc + 1], fb[:, fbc:fbc + 1], ALU.mult, ALU.add,
                  reads=[rp, r_vec, r_fb], writes=[ra])
            i_, ri = AI.next()
            ph.ts("dve", i_[:], a_[:], 1.0 / TWO_PI, None, ALU.mult, reads=[ra], writes=[ri])
            f_, rf = AFR.next()
            ph.copy("dve", f_[:], i_[:], reads=[ri], writes=[rf])
            ph.stt("dve", a_[:], f_[:], -TWO_PI, a_[:], ALU.mult, ALU.add, reads=[rf, ra], writes=[ra])
            ph.ts("dve", a_[:], a_[:], PI_LO, -PI_LO, ALU.min, ALU.max, reads=[ra], writes=[ra])
            ph.act(out, a_[:], AF.Sin, reads=[ra], writes=[r_out])

        for ch in range(4):
            csl = slice(ch * 512, (ch + 1) * 512)
            p, rp = PW.next()
            ph.mm(p[0:64, :], [(w1[0:33, :], feats[0:33, csl])], reads=[r_w1, r_ft], writes=[rp])
            sin_stage(p, rp, VC_FQ0, 0, h1[:, csl], r_h1)
        for ch in range(4):
            csl = slice(ch * 512, (ch + 1) * 512)
            p, rp = PW.next()
            ph.mm(p[0:64, :], [(w2[:], h1[:, csl])], reads=[r_w2, r_h1], writes=[rp])
            sin_stage(p, rp, VC_FQ1, 1, h2[:, csl], r_h2)
        kff = [sb(es, nc, f"{name}_kff{i}", [128, 2048], F32) for i in range(3)]
        kbf = [sb(es, nc, f"{name}_kbf{i}", [128, 2048], F32) for i in range(3)]
        kd = [sb(es, nc, f"{name}_kd{i}", [128, 2048], BF16) for i in range(2)]
        kb2 = [sb(es, nc, f"{name}_kb2{i}", [128, 2048], BF16) for i in range(2)]
        KFF, KBF, KD, KB2 = Rot(kff), Rot(kbf), Rot(kd), Rot(kb2)
        s_pend = []
        s_cnt = [0]

        def s_stage(item):
            a1, ra1, a2, ra2 = item
            first = (s_cnt[0] == 0)
            last = (s_cnt[0] == 15)
            s_cnt[0] += 1

            def sfn(e):
                ins = None
                for cg in range(4):
                    csl = slice(cg * 512, (cg + 1) * 512)
                    e.matmul(pS[cg][:], ones[:], a1[:, csl], start=first, stop=False)
                    ins = e.matmul(pS[cg][:], ones[:], a2[:, csl], start=False, stop=last)
                return ins
            ph.op("pe", sfn, reads=[ra1, ra2, r_ones], writes=[r_S])

        for nt in range(16):
            d_, rd = DEC.next()
            ph.dma("sp", d_[:], cst["decay"][nt * 128:(nt + 1) * 128, :], writes=[rd])
            kf_, rkf = KFF.next()
            kb_, rkb = KBF.next()
            for cg in range(4):
                o, chh = cg // 2, cg % 2
                csl = slice(cg * 512, (cg + 1) * 512)
                p, rp = PW.next()
                c0 = o * 2048 + chh * 512
                ph.mm(p[:], [(h2[:, nt * 128:(nt + 1) * 128], w3[:, c0:c0 + 512])], reads=[r_h2, r_w3], writes=[rp])
                ph.tt("dve", kf_[:, csl], p[:], d_[:, chh * 512:(chh + 1) * 512], ALU.mult, reads=[rp, rd], writes=[rkf])
                p2, rp2 = PW.next()
                c1 = o * 2048 + 1024 + chh * 512
                ph.mm(p2[:], [(h2[:, nt * 128:(nt + 1) * 128], w3[:, c1:c1 + 512])],
                      reads=[r_h2, r_w3], writes=[rp2])
                ph.tt("dve", kb_[:, csl], p2[:], d_[:, chh * 512:(chh + 1) * 512], ALU.mult, reads=[rp2, rd], writes=[rkb])
            if nt == 0:
                ph.memset("dve", kb_[0:1, :], 0.0, writes=[rkb])
            ks_, rks = KT.next()
            ph.tt("pool", ks_[:], kf_[:], kb_[:], ALU.add, reads=[rkf, rkb], writes=[rks])
            kd_, rkd = KD.next()
            ph.tt("dve", kd_[:], kf_[:], kb_[:], ALU.subtract, reads=[rkf, rkb], writes=[rkd])
            a1, ra1 = KA.next()
            ph.act(a1[:], kf_[:], AF.Abs, reads=[rkf], writes=[ra1])
            a2, ra2 = KB2.next()
            ph.act(a2[:], kb_[:], AF.Abs, reads=[rkb], writes=[ra2])
            ph.dma("sp", ktd[nt * 128:(nt + 1) * 128, :], ks_[:], reads=[rks])
            ph.dma("sp", ktd[2048 + nt * 128:2048 + (nt + 1) * 128, :], kd_[:], reads=[rkd])
            s_pend.append((a1, ra1, a2, ra2))
            if len(s_pend) > 1:
                s_stage(s_pend.pop(0))
        while s_pend:
            s_stage(s_pend.pop(0))
        for cg in range(4):
            ph.op("dve", lambda e, cg=cg: e.reciprocal(rs[:, cg * 512:(cg + 1) * 512], pS[cg][:]), reads=[r_S], writes=[r_rs])
        ph.dma("sp", rsd, rs[:], reads=[r_rs])
        ph.run()


def phase_filt_dft(nc, name, ktd, rsd, skip_l, cst, KF):
    ph = Phase(nc, name)
    with ExitStack() as es:
        kres = [sb(es, nc, f"{name}_k{i}", [128, 32, 512], BF16) for i in range(2)]
        fk = [sb(es, nc, f"{name}_fk{i}", [128, 16, 128], BF16) for i in range(4)]
        rs = sb(es, nc, name + "_rs", [128, 2048], F32)
        skb = sb(es, nc, name + "_sk", [128, 2048], F32)
        ot_ = [sb(es, nc, f"{name}_o{i}", [128, 512], F32) for i in range(3)]
        ps = [psb(es, nc, f"{name}_ps{i}", [128, 512]) for i in range(4)]
        pn = psb(es, nc, f"{name}_pn", [128, 512])
        r_rs, r_sk, r_pn = Res(), Res(), Res()
        KR, FKR, OT, PS = Rot(kres), Rot(fk), Rot(ot_), Rot(ps)
        ph.dma("sp", rs[:], rsd, writes=[r_rs])
        ph.dma("sp", skb[:], skip_l, writes=[r_sk])
        kv = ktd.rearrange("(nt p) c -> p nt c", p=128)
        for cg in range(4):
            csl = slice(cg * 512, (cg + 1) * 512)
            k_, rk = KR.next()
            for hf in range(2):
                ph.dma("sp", k_[:, hf * 16:(hf + 1) * 16, :], kv[:, hf * 16:(hf + 1) * 16, csl], writes=[rk])
            fq = []

            def fload(ot):
                f_, rf = FKR.next()
                ph.dma("sp", f_[:], cst["fkl"][ot][:, 0:16, :], writes=[rf])
                fq.append((f_, rf))
            fload(0)
            fload(1)
            for ot in range(32):
                if ot + 2 < 32:
                    fload(ot + 2)
                f_, rf = fq[ot]
                koff = 0 if ot < 16 else 16
                p, rp = PS.next()
                ph.mm(p[:], [(f_[:, nt, :], k_[:, koff + nt, :]) for nt in range(16)], reads=[rf, rk], writes=[rp])
                if ot == 16:
                    ph.mm(pn[0:1, :], [(f_[:, nt, 0:1], k_[:, nt, :]) for nt in range(16)], reads=[rf, rk], writes=[r_pn])
                o_, ro = OT.next()
                ph.tt("dve", o_[:], p[:], rs[:, csl], ALU.mult, reads=[rp, r_rs], writes=[ro])
                if ot < 16:
                    ph.tt("pool", o_[:], o_[:], skb[:, csl], ALU.add, reads=[ro, r_sk], writes=[ro])
                elif ot == 16:
                    ph.tt("dve", o_[0:1, :], pn[0:1, :], rs[0:1, csl], ALU.mult, reads=[r_pn, r_rs, ro], writes=[ro])
                    ph.tt("pool", o_[0:1, :], o_[0:1, :], skb[0:1, csl], ALU.add, reads=[ro, r_sk], writes=[ro])
                ph.dma("sp", KF[ot * 128:(ot + 1) * 128, csl], o_[:], reads=[ro])
        ph.run()


def phase_hyena(nc, name, order, zsrc, KF, gsrc, cst, dst):
    ph = Phase(nc, name)
    with ExitStack() as es:
        zres = sb(es, nc, name + "_z", [128, 16, 1024], BF16)
        Y = sb(es, nc, name + "_Y", [128, 32, 1024], BF16)
        fw = [sb(es, nc, f"{name}_fw{i}", [128, 16, 128], BF16) for i in range(4)]
        kr = [sb(es, nc, f"{name}_kr{i}", [128, 1024], F32) for i in range(2)]
        ki = [sb(es, nc, f"{name}_ki{i}", [128, 1024], F32) for i in range(2)]
        tmp = [sb(es, nc, f"{name}_t{i}", [128, 512], F32) for i in range(8)]
        ps = [psb(es, nc, f"{name}_ps{i}", [128, 512]) for i in range(8)]
        r_z = Res()
        r_Y = [Res() for _ in range(32)]
        FW, KRR, KIR, TMP, PS = Rot(fw), Rot(kr), Rot(ki), Rot(tmp), Rot(ps)
        zv = zsrc.rearrange("(st p) c -> p st c", p=128)
        for q4 in range(4):
            ph.dma("sp", zres[:, q4 * 4:(q4 + 1) * 4, :], zv[:, q4 * 4:(q4 + 1) * 4, :], writes=[r_z])
        for j in range(16):
            fr, rfr = FW.next()
            ph.dma("sp", fr[:], cst["fkl"][j][:, 0:16, :], writes=[rfr])
            fi, rfi = FW.next()
            ph.dma("sp", fi[:], cst["fkl"][16 + j][:, 0:16, :], writes=[rfi])
            kr_, rkr = KRR.next()
            ph.dma("sp", kr_[:], KF[j * 128:(j + 1) * 128, order * 1024:(order + 1) * 1024], writes=[rkr])
            ki_, rki = KIR.next()
            ph.dma("sp", ki_[:], KF[2048 + j * 128:2048 + (j + 1) * 128, order * 1024:(order + 1) * 1024], writes=[rki])
            for cg in range(2):
                csl = slice(cg * 512, (cg + 1) * 512)
                pr, rpr = PS.next()
                ph.mm(pr[:], [(fr[:, st, :], zres[:, st, csl]) for st in range(16)], reads=[rfr, r_z], writes=[rpr])
                pi, rpi = PS.next()
                ph.mm(pi[:], [(fi[:, st, :], zres[:, st, csl]) for st in range(16)], reads=[rfi, r_z], writes=[rpi])
                t1, r1 = TMP.next()
                t2, r2 = TMP.next()
                t3, r3 = TMP.next()
                t4, r4 = TMP.next()
                ph.tt("dve", t1[:], pr[:], kr_[:, csl], ALU.mult, reads=[rpr, rkr], writes=[r1])
                ph.tt("dve", t2[:], pi[:], ki_[:, csl], ALU.mult, reads=[rpi, rki], writes=[r2])
                ph.tt("dve", t3[:], pr[:], ki_[:, csl], ALU.mult, reads=[rpr, rki], writes=[r3])
                ph.tt("dve", t4[:], pi[:], kr_[:, csl], ALU.mult, reads=[rpi, rkr], writes=[r4])
                ph.tt("pool", Y[:, j, csl], t1[:], t2[:], ALU.subtract, reads=[r1, r2], writes=[r_Y[j]])
                ph.tt("pool", Y[:, 16 + j, csl], t3[:], t4[:], ALU.add, reads=[r3, r4], writes=[r_Y[16 + j]])
                if j == 0:
                    ph.copy("pool", Y[0:1, 0, csl], t1[0:1, :], reads=[r1], writes=[r_Y[0]])
                    ph.copy("pool", Y[0:1, 16, csl], t2[0:1, :], reads=[r2], writes=[r_Y[16]])
        if order == 0:
            iw = [sb(es, nc, f"{name}_iw{i}", [128, 32, 128], BF16) for i in range(3)]
            gt = [sb(es, nc, f"{name}_g{i}", [128, 1024], BF16) for i in range(3)]
            zo = [sb(es, nc, f"{name}_zo{i}", [128, 1024], BF16) for i in range(2)]
            IWR, GT, ZO = Rot(iw), Rot(gt), Rot(zo)
            lq = []

            def iload(t16):
                w_, rw = IWR.next()
                ph.dma("sp", w_[:], cst["iwa"][t16], writes=[rw])
                g_, rg = GT.next()
                ph.dma("sp", g_[:], gsrc[t16 * 128:(t16 + 1) * 128, :], writes=[rg])
                lq.append((w_, rw, g_, rg))
            iload(0)
            iload(1)
            for t16 in range(16):
                if t16 + 2 < 16:
                    iload(t16 + 2)
                w_, rw, g_, rg = lq[t16]
                o_, ro = ZO.next()
                for cg in range(2):
                    csl = slice(cg * 512, (cg + 1) * 512)
                    p, rp = PS.next()
                    ph.mm(p[:], [(w_[:, ot, :], Y[:, ot, csl]) for ot in range(32)], reads=[rw] + r_Y, writes=[rp])
                    ph.tt("dve", o_[:, csl], p[:], g_[:, csl], ALU.mult, reads=[rp, rg], writes=[ro])
                ph.dma("sp", dst[t16 * 128:(t16 + 1) * 128, :], o_[:], reads=[ro])
        else:
            iw = [sb(es, nc, f"{name}_iw{i}", [128, 32, 512], BF16) for i in range(1)]
            gt = [sb(es, nc, f"{name}_g{i}", [128, 512], BF16) for i in range(3)]
            zo = [sb(es, nc, f"{name}_zo{i}", [128, 512], BF16) for i in range(2)]
            IWR, GT, ZO = Rot(iw), Rot(gt), Rot(zo)
            for tt in range(4):
                tsl = slice(tt * 512, (tt + 1) * 512)
                w_, rw = IWR.next()
                for hf in range(2):
                    ph.dma("sp", w_[:, hf * 16:(hf + 1) * 16, :], cst["iwb"][tt][:, hf * 16:(hf + 1) * 16, :], writes=[rw])
                gq = []

                def gload(ct, tsl=tsl):
                    g_, rg = GT.next()
                    ph.dma("sp", g_[:], gsrc[ct * 128:(ct + 1) * 128, tsl], writes=[rg])
                    gq.append((g_, rg))
                gload(0)
                gload(1)
                for ct in range(8):
                    if ct + 2 < 8:
                        gload(ct + 2)
                    g_, rg = gq[ct]
                    p, rp = PS.next()
                    ph.mm(p[:], [(Y[:, ot, ct * 128:(ct + 1) * 128], w_[:, ot, :]) for ot in range(32)],
                          reads=[rw] + r_Y, writes=[rp])
                    o_, ro = ZO.next()
                    ph.tt("dve", o_[:], p[:], g_[:], ALU.mult, reads=[rp, rg], writes=[ro])
                    ph.dma("sp", dst[ct * 128:(ct + 1) * 128, tsl], o_[:], reads=[ro])
        ph.run()


WNAMES = ["w_in", "f_w1", "f_w2", "f_w3", "w_proj_attn", "w_proj_hyena", "w_gate", "w_out", "w_up", "w_down"]
WSHAPES = {"w_in": [D, INW], "f_w1": [33, 64], "f_w2": [64, 64], "f_w3": [64, 4096], "w_proj_attn": [512, D],
           "w_proj_hyena": [HWD, D], "w_gate": [D, 2 * D], "w_out": [D, D], "w_up": [D, 2 * DFF], "w_down": [DFF, D]}
CSHAPES = {"ropec": ([128, S], F32), "ropes": ([128, S], F32), "rotm": ([128, 128], BF16), "ident": ([128, 128], BF16),
           "ones": ([128, 128], BF16), "mask": ([128, 256], BF16), "fkl": ([32, 128, 32, 128], BF16),
           "iwa": ([16, 128, 32, 128], BF16), "iwb": ([4, 128, 32, 512], BF16), "feats": ([33, 4096], F32),
           "decay": ([4096, 1024], F32)}


def build(ns, nlayers=DEPTH, stop_after=None, debug=False):
    nc = bass.Bass("TRN2", target_bir_lowering=False)
    kind_s = "ExternalOutput" if debug else "Internal"
    xT = nc.dram_tensor("xT", [ns, D, S], F32, kind="ExternalInput").ap()
    outT = nc.dram_tensor("outT", [ns, D, S], F32, kind="ExternalOutput").ap()
    wd = {n: nc.dram_tensor(n, [DEPTH] + WSHAPES[n], F32, kind="ExternalInput").ap() for n in WNAMES}
    vecs = nc.dram_tensor("vecs", [DEPTH, 128, NV], F32, kind="ExternalInput").ap()
    skipbc = nc.dram_tensor("skipbc", [DEPTH, 128, 2048], F32, kind="ExternalInput").ap()
    cst = {n: nc.dram_tensor(n, sh, dt, kind="ExternalInput").ap() for n, (sh, dt) in CSHAPES.items()}

    def scratch(name, shape, dt):
        return nc.dram_tensor(name, shape, dt, kind=kind_s).ap()
    hT = scratch("s_hT", [D, S], BF16)
    qT = scratch("s_qT", [AW, S], BF16)
    kT = scratch("s_kT", [AW, S], BF16)
    vaug = scratch("s_vaug", [S, 24, 128], BF16)
    z0 = scratch("s_z0", [S, HWD], BF16)
    g0 = scratch("s_g0", [S, HWD], BF16)
    g1T = scratch("s_g1T", [HWD, S], BF16)
    oT = scratch("s_oT", [512, S], BF16)
    ktd = scratch("s_ktd", [4096, 2048], BF16)
    rsd = scratch("s_rsd", [128, 2048], F32)
    KF = scratch("s_KF", [4096, 2048], F32)
    z1 = scratch("s_z1", [S, HWD], BF16)
    ohT = scratch("s_ohT", [HWD, S], BF16)
    mixT = scratch("s_mixT", [D, S], BF16)
    xs = scratch("s_xs", [ns, D, S], F32)
    actT = scratch("s_actT", [DFF, S], BF16)

    def done(tag):
        return stop_after is not None and tag == stop_after

    for l in range(nlayers):
        phase_filt_gen(nc, f"fg{l}", wd["f_w1"][l], wd["f_w2"][l], wd["f_w3"][l], vecs[l], cst, ktd, rsd)
        if done("fgen"):
            return nc
        phase_filt_dft(nc, f"fd{l}", ktd, rsd, skipbc[l], cst, KF)
        if done("fdft"):
            return nc
        for s in range(ns):
            xsrc = xT[s] if l == 0 else xs[s]
            phase_norm(nc, f"n1_{l}_{s}", xsrc, vecs[l], VC_AN, hT)
            phase_inproj(nc, f"ip_{l}_{s}", hT, wd["w_in"][l], vecs[l], cst, qT, kT, vaug, z0, g0, g1T)
            if done("inproj"):
                return nc
            phase_attn(nc, f"at_{l}_{s}", qT, kT, vaug, cst, oT)
            if done("attn"):
                return nc
            phase_hyena(nc, f"hy0_{l}_{s}", 0, z0, KF, g0, cst, z1)
            phase_hyena(nc, f"hy1_{l}_{s}", 1, z1, KF, g1T, cst, ohT)
            if done("hyena"):
                return nc
            phase_gate(nc, f"gt_{l}_{s}", hT, oT, ohT, wd["w_gate"][l], wd["w_proj_attn"][l], wd["w_proj_hyena"][l],
                       vecs[l], mixT)
            if done("gate"):
                return nc
            phase_proj_res(nc, f"op_{l}_{s}", mixT, 16, wd["w_out"][l], xsrc, xs[s], 512, 4)
            if done("outproj"):
                return nc
            phase_norm(nc, f"n2_{l}_{s}", xs[s], vecs[l], VC_FN, hT)
            phase_ffn_up(nc, f"fu_{l}_{s}", hT, wd["w_up"][l], vecs[l], actT)
            if done("ffnup"):
                return nc
            phase_proj_res(nc, f"fdn_{l}_{s}", actT, 44, wd["w_down"][l], xs[s], xs[s], 256, 2)
            if done("ffndown"):
                return nc
    for s in range(ns):
        phase_norm(nc, f"nf_{s}", xs[s], vecs[0], VC_FIN, outT[s], final=True)
    return nc


def host_inputs(inp):
    c = make_consts()
    m = {n: np.ascontiguousarray(inp[n], dtype=np.float32) for n in WNAMES}
    m["vecs"] = np.stack([make_vecs(inp, l) for l in range(DEPTH)])
    sk = np.asarray(inp["hy_skip"], np.float32).reshape(DEPTH, 1, 2048)
    m["skipbc"] = np.ascontiguousarray(np.broadcast_to(sk, (DEPTH, 128, 2048)))
    m.update(c)
    return m


def kernel(**inputs):
    inp = {k: np.asarray(v) for k, v in inputs.items()}
    x = np.asarray(inp["x"], np.float32)
    B = x.shape[0]
    ns = B // NCORES
    shared = host_inputs(inp)
    nc = build(ns)
    in_maps = []
    for c in range(NCORES):
        m = dict(shared)
        m["xT"] = np.ascontiguousarray(x[c * ns:(c + 1) * ns].transpose(0, 2, 1))
        in_maps.append(m)
    res = run_bass_kernel_spmd(nc, in_maps, core_ids=list(range(NCORES)))
    out = np.empty_like(x)
    for c in range(NCORES):
        out[c * ns:(c + 1) * ns] = res.results[c]["outT"].transpose(0, 2, 1)
    return out
```

```python
import math
from contextlib import ExitStack
import numpy as np
import ml_dtypes
import concourse.bass as bass
import concourse.mybir as mybir
from concourse.bass_utils import run_bass_kernel_spmd

F32 = mybir.dt.float32
BF16 = mybir.dt.bfloat16
I32 = mybir.dt.int32
AF = mybir.ActivationFunctionType
ALU = mybir.AluOpType
NPBF = ml_dtypes.bfloat16

D = 2048
S = 2048
DEPTH = 2
AW = 1536
HWD = 1024
DFF = 5632
INW = 7680
EPS = 1e-6
NCORES = 8


class Res:
    __slots__ = ("lw", "rd")

    def __init__(self):
        self.lw = None
        self.rd = []


class Op:
    __slots__ = ("eng", "fn", "deps", "dma", "sem", "val", "prewait")


ENGS = ("sp", "act", "pool", "dve", "pe")
NDS = 8


class Phase:
    def __init__(self, nc, name):
        self.nc = nc
        self.name = name
        self.ops = []

    def op(self, eng, fn, reads=(), writes=(), dma=False):
        o = Op()
        o.eng = eng
        o.fn = fn
        o.dma = dma
        deps = []
        for r in reads:
            if r.lw is not None:
                deps.append(r.lw)
        for w in writes:
            if w.lw is not None:
                deps.append(w.lw)
            deps.extend(w.rd)
        for r in reads:
            r.rd.append(o)
        for w in writes:
            w.lw = o
            w.rd = []
        o.deps = deps
        self.ops.append(o)
        return o

    def dma(self, eng, out, in_, reads=(), writes=()):
        return self.op(eng, lambda e: e.dma_start(out=out, in_=in_), reads, writes, dma=True)

    def act(self, out, in_, func, reads=(), writes=(), **kw):
        return self.op("act", lambda e: e.activation(out=out, in_=in_, func=func, **kw), reads, writes)

    def mm(self, ps, pairs, reads=(), writes=()):
        pairs = list(pairs)

        def fn(e):
            n = len(pairs)
            ins = None
            for i, (l, r) in enumerate(pairs):
                ins = e.matmul(ps, l, r, start=(i == 0), stop=(i == n - 1))
            return ins
        return self.op("pe", fn, reads, writes)

    def mm_multi(self, groups, reads=(), writes=()):
        groups = [(ps, list(pairs)) for ps, pairs in groups]

        def fn(e):
            ins = None
            for ps, pairs in groups:
                n = len(pairs)
                for i, (l, r) in enumerate(pairs):
                    ins = e.matmul(ps, l, r, start=(i == 0), stop=(i == n - 1))
            return ins
        return self.op("pe", fn, reads, writes)

    def tt(self, eng, out, in0, in1, op, reads=(), writes=()):
        return self.op(eng, lambda e: e.tensor_tensor(out, in0, in1, op), reads, writes)

    def ts(self, eng, out, in0, s1, s2, op0, op1=None, reads=(), writes=()):
        if op1 is None:
            return self.op(eng, lambda e: e.tensor_scalar(out, in0, s1, None, op0), reads, writes)
        return self.op(eng, lambda e: e.tensor_scalar(out, in0, s1, s2, op0, op1), reads, writes)

    def stt(self, eng, out, in0, scalar, in1, op0, op1, reads=(), writes=()):
        return self.op(eng, lambda e: e.scalar_tensor_tensor(out, in0, scalar, in1, op0, op1), reads, writes)

    def copy(self, eng, out, in_, reads=(), writes=()):
        if eng == "act":
            return self.act(out, in_, AF.Copy, reads, writes)
        return self.op(eng, lambda e: e.tensor_copy(out, in_), reads, writes)

    def memset(self, eng, ap, val, writes=()):
        return self.op(eng, lambda e: e.memset(ap, val), (), writes)

    def run(self):
        nc = self.nc
        snap = nc.snapshot_sems()
        csem = {e: nc.alloc_semaphore(f"{self.name}_c_{e}") for e in ENGS}
        dq = [e for e in ENGS if any(o.dma and o.eng == e for o in self.ops)]
        dsem = {e: [nc.alloc_semaphore(f"{self.name}_d_{e}{i}") for i in range(NDS)] for e in dq}
        skey = {}
        for e in ENGS:
            skey[id(csem[e])] = ("c", e)
        cnt = {e: 0 for e in ENGS}
        dcnt = {e: 0 for e in ENGS}
        for o in self.ops:
            if o.dma:
                i = dcnt[o.eng]
                dcnt[o.eng] += 1
                o.sem = ("d", o.eng, i % NDS)
                o.val = 16 * (i // NDS + 1)
                o.prewait = (o.sem, 16 * (i // NDS)) if i >= NDS else None
            else:
                cnt[o.eng] += 1
                o.sem = ("c", o.eng)
                o.val = cnt[o.eng]
                o.prewait = None

        def handle(key):
            return csem[key[1]] if key[0] == "c" else dsem[key[1]][key[2]]

        ops = self.ops

        def emit(en, e):
            waited = {}
            for o in ops:
                if o.eng != en:
                    continue
                need = {}
                for d in o.deps:
                    if d.eng == "pe" and en == "pe" and (not d.dma) and (not o.dma):
                        continue
                    if need.get(d.sem, 0) < d.val:
                        need[d.sem] = d.val
                if o.prewait is not None:
                    k, v = o.prewait
                    if need.get(k, 0) < v:
                        need[k] = v
                for k, v in need.items():
                    if waited.get(k, 0) < v:
                        e.wait_ge(handle(k), v)
                        waited[k] = v
                ins = o.fn(e)
                ins.then_inc(handle(o.sem), 16 if o.dma else 1)
            if en == "sp":
                for x in ENGS:
                    if cnt[x] > 0 and not (x == "sp"):
                        e.wait_ge(csem[x], cnt[x])
                for x in dq:
                    for j in range(NDS):
                        n = (dcnt[x] - j + NDS - 1) // NDS
                        if n > 0:
                            e.wait_ge(dsem[x][j], 16 * n)

        with nc.Block() as block:
            @block.sync
            def _(e):
                emit("sp", e)

            @block.scalar
            def _(e):
                emit("act", e)

            @block.gpsimd
            def _(e):
                emit("pool", e)

            @block.vector
            def _(e):
                emit("dve", e)

            @block.tensor
            def _(e):
                emit("pe", e)
        nc.clear_and_free_semaphores(nc.allocated_since(snap))
        nc.all_engine_barrier()
        self.ops = []


class Rot:
    def __init__(self, tiles):
        self.tiles = tiles
        self.res = [Res() for _ in tiles]
        self.i = -1

    def next(self):
        self.i = (self.i + 1) % len(self.tiles)
        return self.tiles[self.i], self.res[self.i]


def sb(es, nc, name, shape, dt):
    return es.enter_context(nc.sbuf_tensor(name, list(shape), dt))


def psb(es, nc, name, shape, dt=F32):
    return es.enter_context(nc.psum_tensor(name, list(shape), dt))


VC_AN = 0
VC_FN = 16
VC_HCW = 32
VC_HCB = 104
VC_FCW = 128
VC_FCB = 260
VC_BG = 304
VC_FIN = 336
VC_FB1 = 352
VC_FQ0 = 353
VC_FQ1 = 354
VC_FB2 = 355
NV = 356


def _cols(v, n):
    return np.ascontiguousarray(v.reshape(n, 128).T)


def make_vecs(inp, l):
    v = np.zeros((128, NV), np.float32)
    v[:, VC_AN:VC_AN + 16] = _cols(inp["attn_norm"][l], 16)
    v[:, VC_FN:VC_FN + 16] = _cols(inp["ffn_norm"][l], 16)
    for t in range(3):
        v[:, VC_HCW + 24 * t:VC_HCW + 24 * (t + 1)] = _cols(inp["hy_conv_w"][l, t], 24)
        v[:, VC_FCW + 44 * t:VC_FCW + 44 * (t + 1)] = _cols(inp["ffn_conv_w"][l, t], 44)
    v[:, VC_HCB:VC_HCB + 24] = _cols(inp["hy_conv_b"][l], 24)
    v[:, VC_FCB:VC_FCB + 44] = _cols(inp["ffn_conv_b"][l], 44)
    v[:, VC_BG:VC_BG + 32] = _cols(inp["b_gate"][l], 32)
    v[:, VC_FIN:VC_FIN + 16] = _cols(inp["final_norm"], 16)
    v[:64, VC_FB1] = inp["f_b1"][l]
    v[:64, VC_FQ0] = inp["f_freq"][l, 0]
    v[:64, VC_FQ1] = inp["f_freq"][l, 1]
    v[:64, VC_FB2] = inp["f_b2"][l]
    return v


_CONSTS = None


def make_consts():
    global _CONSTS
    if _CONSTS is not None:
        return _CONSTS
    c = {}
    pos = np.arange(S, dtype=np.float32)
    inv = (1.0 / (10000.0 ** (np.arange(0, 64, 2, dtype=np.float32) / 64))).astype(np.float32)
    ang = pos[:, None] * inv[None, :]
    ang = np.concatenate([ang, ang], axis=-1)
    cosT = np.cos(ang).T.astype(np.float32)
    sinT = np.sin(ang).T.astype(np.float32)
    c["ropec"] = np.ascontiguousarray(np.concatenate([cosT, cosT], 0))
    c["ropes"] = np.ascontiguousarray(np.concatenate([sinT, sinT], 0))
    R = np.zeros((128, 128), np.float32)
    for m in range(128):
        if (m % 64) < 32:
            R[m + 32, m] = -1.0
        else:
            R[m - 32, m] = 1.0
    c["rotm"] = R.astype(NPBF)
    c["ident"] = np.eye(128, dtype=np.float32).astype(NPBF)
    c["ones"] = np.ones((128, 128), np.float32).astype(NPBF)
    i = np.arange(128)[:, None]
    j = np.arange(256)[None, :]
    c["mask"] = (((j - i) >= 0) & ((j - i) <= 128)).astype(np.float32).astype(NPBF)
    N = 2 * S
    tab_c = np.cos(2 * np.pi * np.arange(N) / N)
    tab_s = np.sin(2 * np.pi * np.arange(N) / N)
    n = np.arange(N, dtype=np.int64)[:, None]
    o = np.arange(N, dtype=np.int64)[None, :]
    f = np.where(o < S, o, o - S)
    idx = (f * n) % N
    FK = np.where(o < S, tab_c[idx], -tab_s[idx])
    FK[:, S] = np.where((np.arange(N) % 2) == 0, 1.0, -1.0)
    c["fkl"] = np.ascontiguousarray(FK.reshape(32, 128, 32, 128).transpose(2, 1, 0, 3)).astype(NPBF)
    del FK
    oo = np.arange(N, dtype=np.int64)[:, None]
    t = np.arange(S, dtype=np.int64)[None, :]
    ff = np.where(oo < S, oo, oo - S)
    idx = (ff * t) % N
    IW = np.where(oo < S, 2.0 * tab_c[idx], -2.0 * tab_s[idx]) / N
    IW[0, :] = 1.0 / N
    IW[S, :] = np.where((np.arange(S) % 2) == 0, 1.0, -1.0) / N
    c["iwa"] = np.ascontiguousarray(IW.reshape(32, 128, 16, 128).transpose(2, 1, 0, 3)).astype(NPBF)
    c["iwb"] = np.ascontiguousarray(IW.reshape(32, 128, 4, 512).transpose(2, 1, 0, 3)).astype(NPBF)
    del IW
    tt = np.linspace(0.0, 1.0, S, dtype=np.float32)[:, None]
    bands = 16
    w = (2.0 * math.pi * np.arange(S, dtype=np.float32)[:, None] / S).astype(np.float32)
    fr = np.linspace(1e-4, bands - 1, bands, dtype=np.float32)[None, :]
    feats = np.concatenate([tt, np.cos(fr * w), -np.sin(fr * w)], axis=-1).astype(np.float32)
    rev = np.zeros_like(feats)
    rev[1:] = feats[:0:-1]
    rev[0] = feats[0]
    c["feats"] = np.ascontiguousarray(np.concatenate([feats.T, rev.T], axis=1))
    max_decay = math.log(1e-2) / 0.3
    min_decay = math.log(1e-2) / 1.5
    deltas = np.linspace(min_decay, max_decay, HWD, dtype=np.float32)
    decay = np.exp(-tt * np.abs(deltas)[None, :]).astype(np.float32)
    drev = np.zeros_like(decay)
    drev[1:] = decay[:0:-1]
    c["decay"] = np.ascontiguousarray(np.concatenate([decay, drev], 0))
    _CONSTS = c
    return c


def phase_norm(nc, name, x_src, vecs_l, gcol, h_dst, final=False):
    ph = Phase(nc, name)
    TW = 256
    NT = S // TW
    with ExitStack() as es:
        vec = sb(es, nc, name + "_vec", [128, NV], F32)
        ones = sb(es, nc, name + "_ones", [128, 128], BF16)
        xt = [sb(es, nc, f"{name}_x{i}", [128, 16, TW], F32) for i in range(4)]
        sq = [sb(es, nc, f"{name}_sq{i}", [128, 16, TW], BF16) for i in range(2)]
        ht = [sb(es, nc, f"{name}_h{i}", [128, 16, TW], F32 if final else BF16) for i in range(3)]
        rms = [sb(es, nc, f"{name}_rms{i}", [128, TW], F32) for i in range(2)]
        ps = [psb(es, nc, f"{name}_ps{i}", [128, 512]) for i in range(2)]
        r_vec, r_ones = Res(), Res()
        X, SQ, H, RMS, PS = Rot(xt), Rot(sq), Rot(ht), Rot(rms), Rot(ps)
        ph.dma("sp", vec[:], vecs_l, writes=[r_vec])
        ph.memset("pool", ones[:], 1.0, writes=[r_ones])
        xs = x_src.rearrange("(kc p) t -> p kc t", p=128)
        hd = h_dst.rearrange("(kc p) t -> p kc t", p=128)
        for tt in range(NT):
            x, rx = X.next()
            ph.dma("sp", x[:], xs[:, :, tt * TW:(tt + 1) * TW], writes=[rx])
            s, rs = SQ.next()
            ph.act(s[:], x[:], AF.Square, reads=[rx], writes=[rs])
            p, rp = PS.next()
            ph.mm(p[:, 0:TW], [(ones[:], s[:, k, :]) for k in range(16)], reads=[r_ones, rs], writes=[rp])
            r, rr = RMS.next()
            ph.act(r[:], p[:, 0:TW], AF.Sqrt, reads=[rp], writes=[rr], scale=1.0 / D, bias=EPS)
            ph.op("dve", lambda e, r=r: e.reciprocal(r[:], r[:]), reads=[rr], writes=[rr])
            h, rh = H.next()
            for k in range(16):
                ph.stt("dve", h[:, k, :], x[:, k, :], vec[:, gcol + k:gcol + k + 1], r[:], ALU.mult, ALU.mult,
                       reads=[rx, rr, r_vec], writes=[rh])
            ph.dma("pool", hd[:, :, tt * TW:(tt + 1) * TW], h[:], reads=[rh])
        ph.run()


def load_resident(ph, eng, dst, src_dram, kc, res):
    sv = src_dram.rearrange("(kc p) t -> p kc t", p=128)
    for tt in range(4):
        ph.dma(eng, dst[:, :, tt * 512:(tt + 1) * 512], sv[:, :, tt * 512:(tt + 1) * 512], writes=[res[tt]])


def wview(w_ap, c0, nc_):
    return w_ap.rearrange("(kc p) n -> p kc n", p=128)[:, :, c0:c0 + nc_]


def conv3(ph, ubuf, r_u, vec, r_vec, c0, c1, c2, cb, tmp, r_tmp, out, r_out):
    ph.ts("dve", tmp[:], ubuf[:, 1:2049], vec[:, c1:c1 + 1], vec[:, cb:cb + 1], ALU.mult, ALU.add,
          reads=[r_u, r_vec], writes=[r_tmp])
    ph.stt("dve", tmp[:], ubuf[:, 0:2048], vec[:, c0:c0 + 1], tmp[:], ALU.mult, ALU.add,
           reads=[r_u, r_vec, r_tmp], writes=[r_tmp])
    ph.stt("dve", out, ubuf[:, 2:2050], vec[:, c2:c2 + 1], tmp[:], ALU.mult, ALU.add,
           reads=[r_u, r_vec, r_tmp], writes=[r_out])


def phase_inproj(nc, name, hT, w_in, vecs_l, cst, qT, kT, vaug, z0, g0, g1T):
    ph = Phase(nc, name)
    with ExitStack() as es:
        hres = sb(es, nc, name + "_h", [128, 16, 2048], BF16)
        vec = sb(es, nc, name + "_vec", [128, NV], F32)
        cosT = sb(es, nc, name + "_cos", [128, 2048], F32)
        sinT = sb(es, nc, name + "_sin", [128, 2048], F32)
        rotm = sb(es, nc, name + "_rot", [128, 128], BF16)
        ident = sb(es, nc, name + "_id", [128, 128], BF16)
        wt = [sb(es, nc, f"{name}_w{i}", [128, 16, 512], BF16) for i in range(3)]
        qs = [sb(es, nc, f"{name}_qs{i}", [128, 512], BF16) for i in range(4)]
        t1 = [sb(es, nc, f"{name}_t1{i}", [128, 512], F32) for i in range(2)]
        t2 = [sb(es, nc, f"{name}_t2{i}", [128, 512], F32) for i in range(2)]
        qo = [sb(es, nc, f"{name}_qo{i}", [128, 2048], BF16) for i in range(3)]
        vo = [sb(es, nc, f"{name}_vo{i}", [128, 8, 128], BF16) for i in range(2)]
        ub = [sb(es, nc, f"{name}_ub{i}", [128, 2050], F32) for i in range(2)]
        tmp = [sb(es, nc, f"{name}_tmp{i}", [128, 2048], F32) for i in range(1)]
        tm = [sb(es, nc, f"{name}_tm{i}", [128, 16, 128], BF16) for i in range(2)]
        ps = [psb(es, nc, f"{name}_ps{i}", [128, 512]) for i in range(4)]
        pr = [psb(es, nc, f"{name}_pr{i}", [128, 512]) for i in range(2)]
        pt = [psb(es, nc, f"{name}_pt{i}", [128, 512], BF16) for i in range(2)]
        r_vec, r_c, r_rot, r_id = Res(), Res(), Res(), Res()
        r_h = [Res() for _ in range(4)]
        W, QS, T1, T2, QO, VO, UB, TMP, TM, PS, PR, PT = (Rot(wt), Rot(qs), Rot(t1), Rot(t2), Rot(qo), Rot(vo),
                                                          Rot(ub), Rot(tmp), Rot(tm), Rot(ps), Rot(pr), Rot(pt))
        ph.dma("sp", vec[:], vecs_l, writes=[r_vec])
        ph.dma("sp", cosT[:], cst["ropec"], writes=[r_c])
        ph.dma("sp", sinT[:], cst["ropes"], writes=[r_c])
        ph.dma("sp", rotm[:], cst["rotm"], writes=[r_rot])
        ph.dma("sp", ident[:], cst["ident"], writes=[r_id])
        load_resident(ph, "sp", hres, hT, 16, r_h)
        for v_, rv in zip(VO.tiles, VO.res):
            ph.memset("pool", v_[:], 1.0, writes=[rv])
        for u_, ru in zip(UB.tiles, UB.res):
            ph.memset("pool", u_[:], 0.0, writes=[ru])

        wcols = ([blk * 512 for blk in range(6)] + [2 * AW + g * 512 for g in range(3)]
                 + [3 * AW + blk * 512 for blk in range(6)])
        wq = []

        def wload(i):
            w, rw = W.next()
            ph.dma("pool", w[:], wview(w_in, wcols[i], 512), writes=[rw])
            wq.append((w, rw))

        def getw(i):
            while len(wq) < min(i + 3, len(wcols)):
                wload(len(wq))
            return wq[i]

        qk_pend = []

        def qk_stage_b(item):
            q, rq, o, ro, tsl, tt, m = item
            p2, rp2 = PR.next()
            ph.mm(p2[:], [(rotm[:], q[:])], reads=[r_rot, rq], writes=[rp2])
            a, ra = T1.next()
            ph.tt("dve", a[:], q[:], cosT[:, tsl], ALU.mult, reads=[rq, r_c], writes=[ra])
            b, rb = T2.next()
            ph.tt("dve", b[:], p2[:], sinT[:, tsl], ALU.mult, reads=[rp2, r_c], writes=[rb])
            ph.tt("pool", o[:, tsl], a[:], b[:], ALU.add, reads=[ra, rb], writes=[ro])
            if tt == 3:
                dst = qT if m < 12 else kT
                mm_ = m % 12
                ph.dma("sp", dst[mm_ * 128:(mm_ + 1) * 128, :], o[:], reads=[ro])

        for blk in range(6):
            w, rw = getw(blk)
            for mi in range(4):
                m = blk * 4 + mi
                o, ro = QO.next()
                for tt in range(4):
                    tsl = slice(tt * 512, (tt + 1) * 512)
                    p, rp = PS.next()
                    ph.mm(p[:], [(w[:, k, mi * 128:(mi + 1) * 128], hres[:, k, tsl]) for k in range(16)],
                          reads=[rw, r_h[tt]], writes=[rp])
                    q, rq = QS.next()
                    ph.copy("act", q[:], p[:], reads=[rp], writes=[rq])
                    qk_pend.append((q, rq, o, ro, tsl, tt, m))
                    if len(qk_pend) > 1:
                        qk_stage_b(qk_pend.pop(0))
        while qk_pend:
            qk_stage_b(qk_pend.pop(0))

        for g in range(3):
            w, rw = getw(6 + g)
            for tk in range(16):
                p, rp = PS.next()
                ph.mm(p[:], [(hres[:, k, tk * 128:(tk + 1) * 128], w[:, k, :]) for k in range(16)],
                      reads=[rw, r_h[tk // 4]], writes=[rp])
                v_, rv = VO.next()
                ph.copy("act", v_[:, :, 0:64], p[:].rearrange("p (h d) -> p h d", d=64), reads=[rp], writes=[rv])
                ph.dma("sp", vaug[tk * 128:(tk + 1) * 128, g * 8:(g + 1) * 8, :], v_[:], reads=[rv])

        u_pend = []

        def u_stage_b(item):
            uc, o, ro = item
            if uc >= 16:
                ph.dma("sp", g1T[(uc - 16) * 128:(uc - 15) * 128, :], o[:], reads=[ro])
                return
            t_, rt = TM.next()
            for q4 in range(4):
                pp, rpp = PT.next()
                for j in range(4):
                    tk = q4 * 4 + j
                    ph.op("pe", lambda e, pp=pp, j=j, tk=tk, o=o: e.transpose(
                        pp[:, j * 128:(j + 1) * 128], o[:, tk * 128:(tk + 1) * 128], ident[:]),
                        reads=[ro, r_id], writes=[rpp])
                ph.copy("act", t_[:, q4 * 4:(q4 + 1) * 4, :], pp[:].rearrange("p (a c) -> p a c", c=128),
                        reads=[rpp], writes=[rt])
            dst = z0 if uc < 8 else g0
            cc = uc % 8
            ph.dma("sp", dst.rearrange("(tk p) c -> p tk c", p=128)[:, :, cc * 128:(cc + 1) * 128], t_[:],
                   reads=[rt])

        for blk in range(6):
            w, rw = getw(9 + blk)
            for mi in range(4):
                uc = blk * 4 + mi
                u_, ru = UB.next()
                for tt in range(4):
                    tsl = slice(tt * 512, (tt + 1) * 512)
                    p, rp = PS.next()
                    ph.mm(p[:], [(w[:, k, mi * 128:(mi + 1) * 128], hres[:, k, tsl]) for k in range(16)],
                          reads=[rw, r_h[tt]], writes=[rp])
                    ph.copy("act", u_[:, 1 + tt * 512:1 + (tt + 1) * 512], p[:], reads=[rp], writes=[ru])
                o, ro = QO.next()
                tp, rtp = TMP.next()
                conv3(ph, u_, ru, vec, r_vec, VC_HCW + uc, VC_HCW + 24 + uc, VC_HCW + 48 + uc, VC_HCB + uc,
                      tp, rtp, o[:], ro)
                u_pend.append((uc, o, ro))
                if len(u_pend) > 1:
                    u_stage_b(u_pend.pop(0))
        while u_pend:
            u_stage_b(u_pend.pop(0))
        ph.run()


def ssl(start, n, step):
    return slice(start, start + (n - 1) * step + 1, step)


def phase_attn(nc, name, qT, kT, vaug, cst, oT):
    ph = Phase(nc, name)
    with ExitStack() as es:
        acc = sb(es, nc, name + "_acc", [128, 8, 2048], F32)
        qg = [sb(es, nc, f"{name}_qg{i}", [64, 8, 2048], BF16) for i in range(1)]
        kg = [sb(es, nc, f"{name}_kg{i}", [64, 8, 2048], BF16) for i in range(1)]
        mask = sb(es, nc, name + "_mask", [128, 2, 256], BF16)
        vt = [sb(es, nc, f"{name}_vt{i}", [128, 8, 128], BF16) for i in range(4)]
        pt = [sb(es, nc, f"{name}_pt{i}", [128, 2, 256], BF16) for i in range(7)]
        rz = [sb(es, nc, f"{name}_rz{i}", [128, 2048], F32) for i in range(2)]
        ot = [sb(es, nc, f"{name}_ot{i}", [128, 2048], BF16) for i in range(2)]
        pss = [psb(es, nc, f"{name}_pss{i}", [128, 512]) for i in range(4)]
        pso = [psb(es, nc, f"{name}_pso{i}", [128, 512]) for i in range(4)]
        r_mask = Res()
        r_acc = [Res() for _ in range(4)]
        QG, KG, VT, PT, RZ, OT, PSS, PSO = Rot(qg), Rot(kg), Rot(vt), Rot(pt), Rot(rz), Rot(ot), Rot(pss), Rot(pso)
        for hh in range(2):
            ph.dma("sp", mask[:, hh, :], cst["mask"], writes=[r_mask])
        for j in range(4):
            ph.memset("pool", acc[:, 2 * j:2 * j + 2, :], 0.0, writes=[r_acc[j]])
        for g, r in enumerate((1, 4, 16)):
            T = S // r
            nkb = T // 128
            q_, rq = QG.next()
            k_, rk = KG.next()
            qv = qT[g * 512:(g + 1) * 512, :].rearrange("(c p) t -> p c t", p=64)
            kv = kT[g * 512:(g + 1) * 512, :].rearrange("(c p) t -> p c t", p=64)
            for half in range(2):
                hs = slice(half * 1024, (half + 1) * 1024)
                ph.dma("sp", q_[:, :, hs], qv[:, :, hs], writes=[rq])
                ph.dma("sp", k_[:, :, hs], kv[:, :, hs], writes=[rk])
            pend = []

            def stage_b(item):
                p, rp, v_, rv, j, nq, qsl = item
                pO, rpO = PSO.next()
                ph.mm_multi([(pO[:, hh * 256:hh * 256 + nq], [(v_[:, 2 * j + hh, :], p[:, hh, 0:nq])])
                             for hh in range(2)], reads=[rv, rp], writes=[rpO])
                pOv = pO[:].rearrange("p (h q) -> p h q", q=256)
                ph.tt("dve", acc[:, 2 * j:2 * j + 2, qsl], acc[:, 2 * j:2 * j + 2, qsl], pOv[:, :, 0:nq], ALU.add,
                      reads=[rpO, r_acc[j]], writes=[r_acc[j]])

            for c in range(r):
                for kb in range(nkb):
                    v_, rv = VT.next()
                    ph.dma("sp", v_[:], vaug[ssl(c + r * 128 * kb, 128, r), g * 8:(g + 1) * 8, :],
                           writes=[rv])
                    jlo = 64 if kb == 0 else 0
                    jhi = 192 if kb == nkb - 1 else 256
                    nq = jhi - jlo
                    q0 = 128 * kb - 64 + jlo
                    qsl = ssl(c + r * q0, nq, r)
                    ksl = ssl(c + r * 128 * kb, 128, r)
                    for j in range(4):
                        pS, rpS = PSS.next()
                        ph.mm_multi([(pS[:, hh * 256:hh * 256 + nq],
                                      [(k_[:, 2 * j + hh, ksl], q_[:, 2 * j + hh, qsl])])
                                     for hh in range(2)], reads=[rq, rk], writes=[rpS])
                        p, rp = PT.next()
                        pSv = pS[:].rearrange("p (h q) -> p h q", q=256)
                        ph.act(p[:, :, 0:nq], pSv[:, :, 0:nq], AF.Exp, reads=[rpS], writes=[rp], scale=0.125)
                        ph.tt("pool", p[:, :, 0:nq], p[:, :, 0:nq], mask[:, :, jlo:jhi], ALU.mult,
                              reads=[rp, r_mask], writes=[rp])
                        pend.append((p, rp, v_, rv, j, nq, qsl))
                        if len(pend) > 3:
                            stage_b(pend.pop(0))
            while pend:
                stage_b(pend.pop(0))
        for h in range(8):
            z_, rzr = RZ.next()
            ph.op("dve", lambda e, z_=z_, h=h: e.reciprocal(z_[0:64, :], acc[64:128, h, :]),
                  reads=[r_acc[h // 2]], writes=[rzr])
            o_, ro = OT.next()
            ph.tt("dve", o_[0:64, :], acc[0:64, h, :], z_[0:64, :], ALU.mult, reads=[r_acc[h // 2], rzr], writes=[ro])
            ph.dma("sp", oT[h * 64:(h + 1) * 64, :], o_[0:64, :], reads=[ro])
        ph.run()


def phase_gate(nc, name, hT, oT, ohT, w_gate, w_pa, w_ph, vecs_l, mixT):
    ph = Phase(nc, name)
    with ExitStack() as es:
        hres = sb(es, nc, name + "_h", [128, 16, 2048], BF16)
        ores = sb(es, nc, name + "_o", [128, 4, 2048], BF16)
        yres = sb(es, nc, name + "_y", [128, 8, 2048], BF16)
        vec = sb(es, nc, name + "_vec", [128, NV], F32)
        wga = [sb(es, nc, f"{name}_wga{i}", [128, 16, 256], BF16) for i in range(2)]
        wgh = [sb(es, nc, f"{name}_wgh{i}", [128, 16, 256], BF16) for i in range(2)]
        wpa = [sb(es, nc, f"{name}_wpa{i}", [128, 4, 256], BF16) for i in range(2)]
        wph = [sb(es, nc, f"{name}_wph{i}", [128, 8, 256], BF16) for i in range(2)]
        ga = [sb(es, nc, f"{name}_ga{i}", [128, 512], F32) for i in range(2)]
        gh = [sb(es, nc, f"{name}_gh{i}", [128, 512], F32) for i in range(2)]
        mo = [sb(es, nc, f"{name}_mo{i}", [128, 2048], BF16) for i in range(2)]
        ps = [psb(es, nc, f"{name}_ps{i}", [128, 512]) for i in range(8)]
        r_vec = Res()
        r_h, r_o, r_y = [Res() for _ in range(4)], [Res() for _ in range(4)], [Res() for _ in range(4)]
        WGA, WGH, WPA, WPH, GA, GH, MO, PS = Rot(wga), Rot(wgh), Rot(wpa), Rot(wph), Rot(ga), Rot(gh), Rot(mo), Rot(ps)
        ph.dma("sp", vec[:], vecs_l, writes=[r_vec])
        load_resident(ph, "sp", hres, hT, 16, r_h)
        load_resident(ph, "sp", ores, oT, 4, r_o)
        load_resident(ph, "sp", yres, ohT, 8, r_y)
        wq = []

        def wload(i):
            c0 = i * 256
            a_, ra = WGA.next()
            ph.dma("pool", a_[:], wview(w_gate, c0, 256), writes=[ra])
            b_, rb = WGH.next()
            ph.dma("pool", b_[:], wview(w_gate, D + c0, 256), writes=[rb])
            c_, rc = WPA.next()
            ph.dma("pool", c_[:], wview(w_pa, c0, 256), writes=[rc])
            d_, rd = WPH.next()
            ph.dma("pool", d_[:], wview(w_ph, c0, 256), writes=[rd])
            wq.append((a_, ra, b_, rb, c_, rc, d_, rd))
        wload(0)
        for blk in range(8):
            if blk + 1 < 8:
                wload(blk + 1)
            a_, ra, b_, rb, c_, rc, d_, rd = wq[blk]
            for mi in range(2):
                m = blk * 2 + mi
                msl = slice(mi * 128, (mi + 1) * 128)
                o_, ro = MO.next()
                for tt in range(4):
                    tsl = slice(tt * 512, (tt + 1) * 512)
                    pA, rpA = PS.next()
                    ph.mm(pA[:], [(a_[:, k, msl], hres[:, k, tsl]) for k in range(16)], reads=[ra, r_h[tt]], writes=[rpA])
                    pB, rpB = PS.next()
                    ph.mm(pB[:], [(b_[:, k, msl], hres[:, k, tsl]) for k in range(16)], reads=[rb, r_h[tt]], writes=[rpB])
                    pC, rpC = PS.next()
                    ph.mm(pC[:], [(c_[:, k, msl], ores[:, k, tsl]) for k in range(4)], reads=[rc, r_o[tt]], writes=[rpC])
                    pD, rpD = PS.next()
                    ph.mm(pD[:], [(d_[:, k, msl], yres[:, k, tsl]) for k in range(8)], reads=[rd, r_y[tt]], writes=[rpD])
                    g1_, rg1 = GA.next()
                    ph.act(g1_[:], pA[:], AF.Sigmoid, reads=[rpA, r_vec], writes=[rg1], bias=vec[:, VC_BG + m:VC_BG + m + 1])
                    g2_, rg2 = GH.next()
                    ph.act(g2_[:], pB[:], AF.Sigmoid, reads=[rpB, r_vec], writes=[rg2],
                           bias=vec[:, VC_BG + 16 + m:VC_BG + 16 + m + 1])
                    ph.tt("dve", g1_[:], g1_[:], pC[:], ALU.mult, reads=[rg1, rpC], writes=[rg1])
                    ph.tt("dve", g2_[:], g2_[:], pD[:], ALU.mult, reads=[rg2, rpD], writes=[rg2])
                    ph.tt("pool", o_[:, tsl], g1_[:], g2_[:], ALU.add, reads=[rg1, rg2], writes=[ro])
                ph.dma("sp", mixT[m * 128:(m + 1) * 128, :], o_[:], reads=[ro])
        ph.run()


def phase_proj_res(nc, name, aT, kc, w, x_src, x_dst, wcols, tpp):
    ph = Phase(nc, name)
    with ExitStack() as es:
        na = 4 if tpp == 4 else tpp + 1
        at = [sb(es, nc, f"{name}_a{i}", [128, kc, 512], BF16) for i in range(na)]
        wt = [sb(es, nc, f"{name}_w{i}", [128, kc, wcols], BF16) for i in range(2)]
        xt = [sb(es, nc, f"{name}_x{i}", [128, 512], F32) for i in range(5)]
        ps = [psb(es, nc, f"{name}_ps{i}", [128, 512]) for i in range(4)]
        AT, WT, XT, PS = Rot(at), Rot(wt), Rot(xt), Rot(ps)
        av = aT.rearrange("(kc p) t -> p kc t", p=128)
        nm = wcols // 128
        npass = 4 // tpp
        aq = []
        xq = []
        items = [(pss_ * tpp + ti, blk, mi) for pss_ in range(npass) for blk in range(D // wcols)
                 for mi in range(nm) for ti in range(tpp)]

        def aload(tt):
            a_, ra = AT.next()
            ph.dma("sp", a_[:], av[:, :, tt * 512:(tt + 1) * 512], writes=[ra])
            aq.append((a_, ra))

        def xload(i):
            tt, blk, mi = items[i]
            m = blk * nm + mi
            x_, rx = XT.next()
            ph.dma("sp", x_[:], x_src[m * 128:(m + 1) * 128, tt * 512:(tt + 1) * 512], writes=[rx])
            xq.append((x_, rx))
        aload(0)
        xload(0)
        xload(1)
        for tt in range(1, min(na, 4)):
            aload(tt)
        it = 0
        for pss_ in range(npass):
            while len(aq) < min(4, (pss_ + 1) * tpp):
                aload(len(aq))
            for blk in range(D // wcols):
                w_, rw = WT.next()
                ph.dma("pool", w_[:], wview(w, blk * wcols, wcols), writes=[rw])
                if blk == 0 and pss_ > 0:
                    while len(aq) < min(4, (pss_ + 1) * tpp + 1):
                        aload(len(aq))
                for mi in range(nm):
                    m = blk * nm + mi
                    for ti in range(tpp):
                        tt = pss_ * tpp + ti
                        tsl = slice(tt * 512, (tt + 1) * 512)
                        a_, ra = aq[tt]
                        if it + 2 < len(items):
                            xload(it + 2)
                        x_, rx = xq[it]
                        it += 1
                        p, rp = PS.next()
                        ph.mm(p[:], [(w_[:, k, mi * 128:(mi + 1) * 128], a_[:, k, :]) for k in range(kc)],
                              reads=[rw, ra], writes=[rp])
                        ph.tt("dve", x_[:], x_[:], p[:], ALU.add, reads=[rx, rp], writes=[rx])
                        ph.dma("sp", x_dst[m * 128:(m + 1) * 128, tsl], x_[:], reads=[rx])
        ph.run()


GELU_C = 1.5957691216057308


def phase_ffn_up(nc, name, hT, w_up, vecs_l, actT):
    ph = Phase(nc, name)
    with ExitStack() as es:
        hres = sb(es, nc, name + "_h", [128, 16, 2048], BF16)
        vec = sb(es, nc, name + "_vec", [128, NV], F32)
        wa = [sb(es, nc, f"{name}_wa{i}", [128, 16, 256], BF16) for i in range(3)]
        wg = [sb(es, nc, f"{name}_wg{i}", [128, 16, 256], BF16) for i in range(3)]
        ab = [sb(es, nc, f"{name}_ab{i}", [128, 2050], F32) for i in range(2)]
        gb = [sb(es, nc, f"{name}_gb{i}", [128, 2048], F32) for i in range(2)]
        tmp = [sb(es, nc, f"{name}_tmp{i}", [128, 2048], F32) for i in range(1)]
        av = [sb(es, nc, f"{name}_av{i}", [128, 2048], F32) for i in range(2)]
        sv = [sb(es, nc, f"{name}_sv{i}", [128, 2048], F32) for i in range(2)]
        ob = [sb(es, nc, f"{name}_ob{i}", [128, 2048], BF16) for i in range(2)]
        ps = [psb(es, nc, f"{name}_ps{i}", [128, 512]) for i in range(8)]
        r_vec = Res()
        r_h = [Res() for _ in range(4)]
        WA, WG, AB, GB, TMP, AV, SV, OB, PS = (Rot(wa), Rot(wg), Rot(ab), Rot(gb), Rot(tmp), Rot(av), Rot(sv),
                                                Rot(ob), Rot(ps))
        ph.dma("sp", vec[:], vecs_l, writes=[r_vec])
        load_resident(ph, "sp", hres, hT, 16, r_h)
        for u_, ru in zip(AB.tiles, AB.res):
            ph.memset("pool", u_[:], 0.0, writes=[ru])
        wq = []

        def wload(i):
            a_, ra = WA.next()
            ph.dma("pool", a_[:], wview(w_up, i * 256, 256), writes=[ra])
            g_, rg = WG.next()
            ph.dma("pool", g_[:], wview(w_up, DFF + i * 256, 256), writes=[rg])
            wq.append((a_, ra, g_, rg))
        wload(0)
        wload(1)
        for blk in range(22):
            if blk + 2 < 22:
                wload(blk + 2)
            a_, ra, g_, rg = wq[blk]
            for mi in range(2):
                j = blk * 2 + mi
                msl = slice(mi * 128, (mi + 1) * 128)
                u_, ru = AB.next()
                gg, rgg = GB.next()
                for tt in range(4):
                    tsl = slice(tt * 512, (tt + 1) * 512)
                    p, rp = PS.next()
                    ph.mm(p[:], [(a_[:, k, msl], hres[:, k, tsl]) for k in range(16)], reads=[ra, r_h[tt]], writes=[rp])
                    ph.copy("act", u_[:, 1 + tt * 512:1 + (tt + 1) * 512], p[:], reads=[rp], writes=[ru])
                    p2, rp2 = PS.next()
                    ph.mm(p2[:], [(g_[:, k, msl], hres[:, k, tsl]) for k in range(16)], reads=[rg, r_h[tt]], writes=[rp2])
                    ph.copy("act", gg[:, tsl], p2[:], reads=[rp2], writes=[rgg])
                a2, ra2 = AV.next()
                tp, rtp = TMP.next()
                conv3(ph, u_, ru, vec, r_vec, VC_FCW + j, VC_FCW + 44 + j, VC_FCW + 88 + j, VC_FCB + j, tp, rtp, a2[:], ra2)
                s_, rs = SV.next()
                ph.act(s_[:], a2[:], AF.Square, reads=[ra2], writes=[rs])
                ph.ts("pool", s_[:], s_[:], 0.044715, 1.0, ALU.mult, ALU.add, reads=[rs], writes=[rs])
                ph.tt("pool", s_[:], s_[:], a2[:], ALU.mult, reads=[rs, ra2], writes=[rs])
                ph.act(s_[:], s_[:], AF.Sigmoid, reads=[rs], writes=[rs], scale=GELU_C)
                ph.tt("pool", s_[:], s_[:], a2[:], ALU.mult, reads=[rs, ra2], writes=[rs])
                o_, ro = OB.next()
                ph.tt("dve", o_[:], s_[:], gg[:], ALU.mult, reads=[rs, rgg], writes=[ro])
                ph.dma("sp", actT[j * 128:(j + 1) * 128, :], o_[:], reads=[ro])
        ph.run()


TWO_PI = 2.0 * math.pi
PI_LO = 3.1415925


def phase_filt_gen(nc, name, f_w1, f_w2, f_w3, vecs_l, cst, ktd, rsd):
    ph = Phase(nc, name)
    with ExitStack() as es:
        vec = sb(es, nc, name + "_vec", [128, NV], F32)
        feats = sb(es, nc, name + "_ft", [64, 4096], F32)
        w1 = sb(es, nc, name + "_w1", [64, 64], F32)
        w2 = sb(es, nc, name + "_w2", [64, 64], F32)
        w3 = sb(es, nc, name + "_w3", [64, 4096], F32)
        h1 = sb(es, nc, name + "_h1", [64, 4096], F32)
        h2 = sb(es, nc, name + "_h2", [64, 4096], F32)
        fb = sb(es, nc, name + "_fb", [64, 2], F32)
        ones = sb(es, nc, name + "_ones", [128, 128], BF16)
        arg = [sb(es, nc, f"{name}_arg{i}", [64, 512], F32) for i in range(2)]
        ai = [sb(es, nc, f"{name}_ai{i}", [64, 512], I32) for i in range(2)]
        af = [sb(es, nc, f"{name}_af{i}", [64, 512], F32) for i in range(2)]
        dec = [sb(es, nc, f"{name}_dec{i}", [128, 1024], F32) for i in range(2)]
        kt = [sb(es, nc, f"{name}_kt{i}", [128, 2048], BF16) for i in range(2)]
        ka = [sb(es, nc, f"{name}_ka{i}", [128, 2048], BF16) for i in range(2)]
        rs = sb(es, nc, name + "_rs", [128, 2048], F32)
        pS = [psb(es, nc, f"{name}_pS{i}", [128, 512]) for i in range(4)]
        pw = [psb(es, nc, f"{name}_pw{i}", [128, 512]) for i in range(4)]
        r_vec, r_ft, r_w1, r_w2, r_w3, r_h1, r_h2, r_fb, r_ones, r_S, r_rs = (Res() for _ in range(11))
        ARG, AI, AFR, DEC, KT, KA, PW = Rot(arg), Rot(ai), Rot(af), Rot(dec), Rot(kt), Rot(ka), Rot(pw)
        ph.dma("sp", vec[:], vecs_l, writes=[r_vec])
        ph.dma("sp", feats[0:33, :], cst["feats"], writes=[r_ft])
        ph.dma("sp", w1[0:33, :], f_w1, writes=[r_w1])
        ph.dma("sp", w2[:], f_w2, writes=[r_w2])
        ph.dma("sp", w3[:], f_w3, writes=[r_w3])
        ph.memset("pool", ones[:], 1.0, writes=[r_ones])
        ph.tt("dve", fb[:, 0:1], vec[0:64, VC_FQ0:VC_FQ0 + 1], vec[0:64, VC_FB1:VC_FB1 + 1], ALU.mult,
              reads=[r_vec], writes=[r_fb])
        ph.tt("dve", fb[:, 1:2], vec[0:64, VC_FQ1:VC_FQ1 + 1], vec[0:64, VC_FB2:VC_FB2 + 1], ALU.mult,
              reads=[r_vec], writes=[r_fb])

        def sin_stage(p, rp, fqc, fbc, out, r_out):
            a_, ra = ARG.next()
            ph.ts("dve", a_[:], p[0:64, :], vec[0:64, fqc:fqc + 1], fb[:, fbc:fbc + 1], ALU.mult, ALU.add,
                  reads=[rp, r_vec, r_fb], writes=[ra])
            i_, ri = AI.next()
            ph.ts("dve", i_[:], a_[:], 1.0 / TWO_PI, None, ALU.mult, reads=[ra], writes=[ri])
            f_, rf = AFR.next()
            ph.copy("dve", f_[:], i_[:], reads=[ri], writes=[rf])
            ph.stt("dve", a_[:], f_[:], -TWO_PI, a_[:], ALU.mult, ALU.add, reads=[rf, ra], writes=[ra])
            ph.ts("dve", a_[:], a_[:], PI_LO, -PI_LO, ALU.min, ALU.max, reads=[ra], writes=[ra])
            ph.act(out, a_[:], AF.Sin, reads=[ra], writes=[r_out])

        for ch in range(4):
            csl = slice(ch * 512, (ch + 1) * 512)
            p, rp = PW.next()
            ph.mm(p[0:64, :], [(w1[0:33, :], feats[0:33, csl])], reads=[r_w1, r_ft], writes=[rp])
            sin_stage(p, rp, VC_FQ0, 0, h1[:, csl], r_h1)
        for ch in range(4):
            csl = slice(ch * 512, (ch + 1) * 512)
            p, rp = PW.next()
            ph.mm(p[0:64, :], [(w2[:], h1[:, csl])], reads=[r_w2, r_h1], writes=[rp])
            sin_stage(p, rp, VC_FQ1, 1, h2[:, csl], r_h2)
        kff = [sb(es, nc, f"{name}_kff{i}", [128, 2048], F32) for i in range(3)]
        kbf = [sb(es, nc, f"{name}_kbf{i}", [128, 2048], F32) for i in range(3)]
        kd = [sb(es, nc, f"{name}_kd{i}", [128, 2048], BF16) for i in range(2)]
        kb2 = [sb(es, nc, f"{name}_kb2{i}", [128, 2048], BF16) for i in range(2)]
        KFF, KBF, KD, KB2 = Rot(kff), Rot(kbf), Rot(kd), Rot(kb2)
        s_pend = []
        s_cnt = [0]

        def s_stage(item):
            a1, ra1, a2, ra2 = item
            first = (s_cnt[0] == 0)
            last = (s_cnt[0] == 15)
            s_cnt[0] += 1

            def sfn(e):
                ins = None
                for cg in range(4):
                    csl = slice(cg * 512, (cg + 1) * 512)
                    e.matmul(pS[cg][:], ones[:], a1[:, csl], start=first, stop=False)
                    ins = e.matmul(pS[cg][:], ones[:], a2[:, csl], start=False, stop=last)
                return ins
            ph.op("pe", sfn, reads=[ra1, ra2, r_ones], writes=[r_S])

        for nt in range(16):
            d_, rd = DEC.next()
            ph.dma("sp", d_[:], cst["decay"][nt * 128:(nt + 1) * 128, :], writes=[rd])
            kf_, rkf = KFF.next()
            kb_, rkb = KBF.next()
            for cg in range(4):
                o, chh = cg // 2, cg % 2
                csl = slice(cg * 512, (cg + 1) * 512)
                p, rp = PW.next()
                c0 = o * 2048 + chh * 512
                ph.mm(p[:], [(h2[:, nt * 128:(nt + 1) * 128], w3[:, c0:c0 + 512])], reads=[r_h2, r_w3], writes=[rp])
                ph.tt("dve", kf_[:, csl], p[:], d_[:, chh * 512:(chh + 1) * 512], ALU.mult, reads=[rp, rd], writes=[rkf])
                p2, rp2 = PW.next()
                c1 = o * 2048 + 1024 + chh * 512
                ph.mm(p2[:], [(h2[:, nt * 128:(nt + 1) * 128], w3[:, c1:c1 + 512])],
                      reads=[r_h2, r_w3], writes=[rp2])
                ph.tt("dve", kb_[:, csl], p2[:], d_[:, chh * 512:(chh + 1) * 512], ALU.mult, reads=[rp2, rd], writes=[rkb])
            if nt == 0:
                ph.memset("dve", kb_[0:1, :], 0.0, writes=[rkb])
            ks_, rks = KT.next()
            ph.tt("pool", ks_[:], kf_[:], kb_[:], ALU.add, reads=[rkf, rkb], writes=[rks])
            kd_, rkd = KD.next()
            ph.tt("dve", kd_[:], kf_[:], kb_[:], ALU.subtract, reads=[rkf, rkb], writes=[rkd])
            a1, ra1 = KA.next()
            ph.act(a1[:], kf_[:], AF.Abs, reads=[rkf], writes=[ra1])
            a2, ra2 = KB2.next()
            ph.act(a2[:], kb_[:], AF.Abs, reads=[rkb], writes=[ra2])
            ph.dma("sp", ktd[nt * 128:(nt + 1) * 128, :], ks_[:], reads=[rks])
            ph.dma("sp", ktd[2048 + nt * 128:2048 + (nt + 1) * 128, :], kd_[:], reads=[rkd])
            s_pend.append((a1, ra1, a2, ra2))
            if len(s_pend) > 1:
                s_stage(s_pend.pop(0))
        while s_pend:
            s_stage(s_pend.pop(0))
        for cg in range(4):
            ph.op("dve", lambda e, cg=cg: e.reciprocal(rs[:, cg * 512:(cg + 1) * 512], pS[cg][:]), reads=[r_S], writes=[r_rs])
        ph.dma("sp", rsd, rs[:], reads=[r_rs])
        ph.run()


def phase_filt_dft(nc, name, ktd, rsd, skip_l, cst, KF):
    ph = Phase(nc, name)
    with ExitStack() as es:
        kres = [sb(es, nc, f"{name}_k{i}", [128, 32, 512], BF16) for i in range(2)]
        fk = [sb(es, nc, f"{name}_fk{i}", [128, 16, 128], BF16) for i in range(4)]
        rs = sb(es, nc, name + "_rs", [128, 2048], F32)
        skb = sb(es, nc, name + "_sk", [128, 2048], F32)
        ot_ = [sb(es, nc, f"{name}_o{i}", [128, 512], F32) for i in range(3)]
        ps = [psb(es, nc, f"{name}_ps{i}", [128, 512]) for i in range(4)]
        pn = psb(es, nc, f"{name}_pn", [128, 512])
        r_rs, r_sk, r_pn = Res(), Res(), Res()
        KR, FKR, OT, PS = Rot(kres), Rot(fk), Rot(ot_), Rot(ps)
        ph.dma("sp", rs[:], rsd, writes=[r_rs])
        ph.dma("sp", skb[:], skip_l, writes=[r_sk])
        kv = ktd.rearrange("(nt p) c -> p nt c", p=128)
        for cg in range(4):
            csl = slice(cg * 512, (cg + 1) * 512)
            k_, rk = KR.next()
            for hf in range(2):
                ph.dma("sp", k_[:, hf * 16:(hf + 1) * 16, :], kv[:, hf * 16:(hf + 1) * 16, csl], writes=[rk])
            fq = []

            def fload(ot):
                f_, rf = FKR.next()
                ph.dma("sp", f_[:], cst["fkl"][ot][:, 0:16, :], writes=[rf])
                fq.append((f_, rf))
            fload(0)
            fload(1)
            for ot in range(32):
                if ot + 2 < 32:
                    fload(ot + 2)
                f_, rf = fq[ot]
                koff = 0 if ot < 16 else 16
                p, rp = PS.next()
                ph.mm(p[:], [(f_[:, nt, :], k_[:, koff + nt, :]) for nt in range(16)], reads=[rf, rk], writes=[rp])
                if ot == 16:
                    ph.mm(pn[0:1, :], [(f_[:, nt, 0:1], k_[:, nt, :]) for nt in range(16)], reads=[rf, rk], writes=[r_pn])
                o_, ro = OT.next()
                ph.tt("dve", o_[:], p[:], rs[:, csl], ALU.mult, reads=[rp, r_rs], writes=[ro])
                if ot < 16:
                    ph.tt("pool", o_[:], o_[:], skb[:, csl], ALU.add, reads=[ro, r_sk], writes=[ro])
                elif ot == 16:
                    ph.tt("dve", o_[0:1, :], pn[0:1, :], rs[0:1, csl], ALU.mult, reads=[r_pn, r_rs, ro], writes=[ro])
                    ph.tt("pool", o_[0:1, :], o_[0:1, :], skb[0:1, csl], ALU.add, reads=[ro, r_sk], writes=[ro])
                ph.dma("sp", KF[ot * 128:(ot + 1) * 128, csl], o_[:], reads=[ro])
        ph.run()


def phase_hyena(nc, name, order, zsrc, KF, gsrc, cst, dst):
    ph = Phase(nc, name)
    with ExitStack() as es:
        zres = sb(es, nc, name + "_z", [128, 16, 1024], BF16)
        Y = sb(es, nc, name + "_Y", [128, 32, 1024], BF16)
        fw = [sb(es, nc, f"{name}_fw{i}", [128, 16, 128], BF16) for i in range(4)]
        kr = [sb(es, nc, f"{name}_kr{i}", [128, 1024], F32) for i in range(2)]
        ki = [sb(es, nc, f"{name}_ki{i}", [128, 1024], F32) for i in range(2)]
        tmp = [sb(es, nc, f"{name}_t{i}", [128, 512], F32) for i in range(8)]
        ps = [psb(es, nc, f"{name}_ps{i}", [128, 512]) for i in range(8)]
        r_z = Res()
        r_Y = [Res() for _ in range(32)]
        FW, KRR, KIR, TMP, PS = Rot(fw), Rot(kr), Rot(ki), Rot(tmp), Rot(ps)
        zv = zsrc.rearrange("(st p) c -> p st c", p=128)
        for q4 in range(4):
            ph.dma("sp", zres[:, q4 * 4:(q4 + 1) * 4, :], zv[:, q4 * 4:(q4 + 1) * 4, :], writes=[r_z])
        for j in range(16):
            fr, rfr = FW.next()
            ph.dma("sp", fr[:], cst["fkl"][j][:, 0:16, :], writes=[rfr])
            fi, rfi = FW.next()
            ph.dma("sp", fi[:], cst["fkl"][16 + j][:, 0:16, :], writes=[rfi])
            kr_, rkr = KRR.next()
            ph.dma("sp", kr_[:], KF[j * 128:(j + 1) * 128, order * 1024:(order + 1) * 1024], writes=[rkr])
            ki_, rki = KIR.next()
            ph.dma("sp", ki_[:], KF[2048 + j * 128:2048 + (j + 1) * 128, order * 1024:(order + 1) * 1024], writes=[rki])
            for cg in range(2):
                csl = slice(cg * 512, (cg + 1) * 512)
                pr, rpr = PS.next()
                ph.mm(pr[:], [(fr[:, st, :], zres[:, st, csl]) for st in range(16)], reads=[rfr, r_z], writes=[rpr])
                pi, rpi = PS.next()
                ph.mm(pi[:], [(fi[:, st, :], zres[:, st, csl]) for st in range(16)], reads=[rfi, r_z], writes=[rpi])
                t1, r1 = TMP.next()
                t2, r2 = TMP.next()
                t3, r3 = TMP.next()
                t4, r4 = TMP.next()
                ph.tt("dve", t1[:], pr[:], kr_[:, csl], ALU.mult, reads=[rpr, rkr], writes=[r1])
                ph.tt("dve", t2[:], pi[:], ki_[:, csl], ALU.mult, reads=[rpi, rki], writes=[r2])
                ph.tt("dve", t3[:], pr[:], ki_[:, csl], ALU.mult, reads=[rpr, rki], writes=[r3])
                ph.tt("dve", t4[:], pi[:], kr_[:, csl], ALU.mult, reads=[rpi, rkr], writes=[r4])
                ph.tt("pool", Y[:, j, csl], t1[:], t2[:], ALU.subtract, reads=[r1, r2], writes=[r_Y[j]])
                ph.tt("pool", Y[:, 16 + j, csl], t3[:], t4[:], ALU.add, reads=[r3, r4], writes=[r_Y[16 + j]])
                if j == 0:
                    ph.copy("pool", Y[0:1, 0, csl], t1[0:1, :], reads=[r1], writes=[r_Y[0]])
                    ph.copy("pool", Y[0:1, 16, csl], t2[0:1, :], reads=[r2], writes=[r_Y[16]])
        if order == 0:
            iw = [sb(es, nc, f"{name}_iw{i}", [128, 32, 128], BF16) for i in range(3)]
            gt = [sb(es, nc, f"{name}_g{i}", [128, 1024], BF16) for i in range(3)]
            zo = [sb(es, nc, f"{name}_zo{i}", [128, 1024], BF16) for i in range(2)]
            IWR, GT, ZO = Rot(iw), Rot(gt), Rot(zo)
            lq = []

            def iload(t16):
                w_, rw = IWR.next()
                ph.dma("sp", w_[:], cst["iwa"][t16], writes=[rw])
                g_, rg = GT.next()
                ph.dma("sp", g_[:], gsrc[t16 * 128:(t16 + 1) * 128, :], writes=[rg])
                lq.append((w_, rw, g_, rg))
            iload(0)
            iload(1)
            for t16 in range(16):
                if t16 + 2 < 16:
                    iload(t16 + 2)
                w_, rw, g_, rg = lq[t16]
                o_, ro = ZO.next()
                for cg in range(2):
                    csl = slice(cg * 512, (cg + 1) * 512)
                    p, rp = PS.next()
                    ph.mm(p[:], [(w_[:, ot, :], Y[:, ot, csl]) for ot in range(32)], reads=[rw] + r_Y, writes=[rp])
                    ph.tt("dve", o_[:, csl], p[:], g_[:, csl], ALU.mult, reads=[rp, rg], writes=[ro])
                ph.dma("sp", dst[t16 * 128:(t16 + 1) * 128, :], o_[:], reads=[ro])
        else:
            iw = [sb(es, nc, f"{name}_iw{i}", [128, 32, 512], BF16) for i in range(1)]
            gt = [sb(es, nc, f"{name}_g{i}", [128, 512], BF16) for i in range(3)]
            zo = [sb(es, nc, f"{name}_zo{i}", [128, 512], BF16) for i in range(2)]
            IWR, GT, ZO = Rot(iw), Rot(gt), Rot(zo)
            for tt in range(4):
                tsl = slice(tt * 512, (tt + 1) * 512)
                w_, rw = IWR.next()
                for hf in range(2):
                    ph.dma("sp", w_[:, hf * 16:(hf + 1) * 16, :], cst["iwb"][tt][:, hf * 16:(hf + 1) * 16, :], writes=[rw])
                gq = []

                def gload(ct, tsl=tsl):
                    g_, rg = GT.next()
                    ph.dma("sp", g_[:], gsrc[ct * 128:(ct + 1) * 128, tsl], writes=[rg])
                    gq.append((g_, rg))
                gload(0)
                gload(1)
                for ct in range(8):
                    if ct + 2 < 8:
                        gload(ct + 2)
                    g_, rg = gq[ct]
                    p, rp = PS.next()
                    ph.mm(p[:], [(Y[:, ot, ct * 128:(ct + 1) * 128], w_[:, ot, :]) for ot in range(32)],
                          reads=[rw] + r_Y, writes=[rp])
                    o_, ro = ZO.next()
                    ph.tt("dve", o_[:], p[:], g_[:], ALU.mult, reads=[rp, rg], writes=[ro])
                    ph.dma("sp", dst[ct * 128:(ct + 1) * 128, tsl], o_[:], reads=[ro])
        ph.run()


WNAMES = ["w_in", "f_w1", "f_w2", "f_w3", "w_proj_attn", "w_proj_hyena", "w_gate", "w_out", "w_up", "w_down"]
WSHAPES = {"w_in": [D, INW], "f_w1": [33, 64], "f_w2": [64, 64], "f_w3": [64, 4096], "w_proj_attn": [512, D],
           "w_proj_hyena": [HWD, D], "w_gate": [D, 2 * D], "w_out": [D, D], "w_up": [D, 2 * DFF], "w_down": [DFF, D]}
CSHAPES = {"ropec": ([128, S], F32), "ropes": ([128, S], F32), "rotm": ([128, 128], BF16), "ident": ([128, 128], BF16),
           "ones": ([128, 128], BF16), "mask": ([128, 256], BF16), "fkl": ([32, 128, 32, 128], BF16),
           "iwa": ([16, 128, 32, 128], BF16), "iwb": ([4, 128, 32, 512], BF16), "feats": ([33, 4096], F32),
           "decay": ([4096, 1024], F32)}


def build(ns, nlayers=DEPTH, stop_after=None, debug=False):
    nc = bass.Bass("TRN2", target_bir_lowering=False)
    kind_s = "ExternalOutput" if debug else "Internal"
    xT = nc.dram_tensor("xT", [ns, D, S], F32, kind="ExternalInput").ap()
    outT = nc.dram_tensor("outT", [ns, D, S], F32, kind="ExternalOutput").ap()
    wd = {n: nc.dram_tensor(n, [DEPTH] + WSHAPES[n], F32, kind="ExternalInput").ap() for n in WNAMES}
    vecs = nc.dram_tensor("vecs", [DEPTH, 128, NV], F32, kind="ExternalInput").ap()
    skipbc = nc.dram_tensor("skipbc", [DEPTH, 128, 2048], F32, kind="ExternalInput").ap()
    cst = {n: nc.dram_tensor(n, sh, dt, kind="ExternalInput").ap() for n, (sh, dt) in CSHAPES.items()}

    def scratch(name, shape, dt):
        return nc.dram_tensor(name, shape, dt, kind=kind_s).ap()
    hT = scratch("s_hT", [D, S], BF16)
    qT = scratch("s_qT", [AW, S], BF16)
    kT = scratch("s_kT", [AW, S], BF16)
    vaug = scratch("s_vaug", [S, 24, 128], BF16)
    z0 = scratch("s_z0", [S, HWD], BF16)
    g0 = scratch("s_g0", [S, HWD], BF16)
    g1T = scratch("s_g1T", [HWD, S], BF16)
    oT = scratch("s_oT", [512, S], BF16)
    ktd = scratch("s_ktd", [4096, 2048], BF16)
    rsd = scratch("s_rsd", [128, 2048], F32)
    KF = scratch("s_KF", [4096, 2048], F32)
    z1 = scratch("s_z1", [S, HWD], BF16)
    ohT = scratch("s_ohT", [HWD, S], BF16)
    mixT = scratch("s_mixT", [D, S], BF16)
    xs = scratch("s_xs", [ns, D, S], F32)
    actT = scratch("s_actT", [DFF, S], BF16)

    def done(tag):
        return stop_after is not None and tag == stop_after

    for l in range(nlayers):
        phase_filt_gen(nc, f"fg{l}", wd["f_w1"][l], wd["f_w2"][l], wd["f_w3"][l], vecs[l], cst, ktd, rsd)
        if done("fgen"):
            return nc
        phase_filt_dft(nc, f"fd{l}", ktd, rsd, skipbc[l], cst, KF)
        if done("fdft"):
            return nc
        for s in range(ns):
            xsrc = xT[s] if l == 0 else xs[s]
            phase_norm(nc, f"n1_{l}_{s}", xsrc, vecs[l], VC_AN, hT)
            phase_inproj(nc, f"ip_{l}_{s}", hT, wd["w_in"][l], vecs[l], cst, qT, kT, vaug, z0, g0, g1T)
            if done("inproj"):
                return nc
            phase_attn(nc, f"at_{l}_{s}", qT, kT, vaug, cst, oT)
            if done("attn"):
                return nc
            phase_hyena(nc, f"hy0_{l}_{s}", 0, z0, KF, g0, cst, z1)
            phase_hyena(nc, f"hy1_{l}_{s}", 1, z1, KF, g1T, cst, ohT)
            if done("hyena"):
                return nc
            phase_gate(nc, f"gt_{l}_{s}", hT, oT, ohT, wd["w_gate"][l], wd["w_proj_attn"][l], wd["w_proj_hyena"][l],
                       vecs[l], mixT)
            if done("gate"):
                return nc
            phase_proj_res(nc, f"op_{l}_{s}", mixT, 16, wd["w_out"][l], xsrc, xs[s], 512, 4)
            if done("outproj"):
                return nc
            phase_norm(nc, f"n2_{l}_{s}", xs[s], vecs[l], VC_FN, hT)
            phase_ffn_up(nc, f"fu_{l}_{s}", hT, wd["w_up"][l], vecs[l], actT)
            if done("ffnup"):
                return nc
            phase_proj_res(nc, f"fdn_{l}_{s}", actT, 44, wd["w_down"][l], xs[s], xs[s], 256, 2)
            if done("ffndown"):
                return nc
    for s in range(ns):
        phase_norm(nc, f"nf_{s}", xs[s], vecs[0], VC_FIN, outT[s], final=True)
    return nc


def host_inputs(inp):
    c = make_consts()
    m = {n: np.ascontiguousarray(inp[n], dtype=np.float32) for n in WNAMES}
    m["vecs"] = np.stack([make_vecs(inp, l) for l in range(DEPTH)])
    sk = np.asarray(inp["hy_skip"], np.float32).reshape(DEPTH, 1, 2048)
    m["skipbc"] = np.ascontiguousarray(np.broadcast_to(sk, (DEPTH, 128, 2048)))
    m.update(c)
    return m


def kernel(**inputs):
    inp = {k: np.asarray(v) for k, v in inputs.items()}
    x = np.asarray(inp["x"], np.float32)
    B = x.shape[0]
    ns = B // NCORES
    shared = host_inputs(inp)
    nc = build(ns)
    in_maps = []
    for c in range(NCORES):
        m = dict(shared)
        m["xT"] = np.ascontiguousarray(x[c * ns:(c + 1) * ns].transpose(0, 2, 1))
        in_maps.append(m)
    res = run_bass_kernel_spmd(nc, in_maps, core_ids=list(range(NCORES)))
    out = np.empty_like(x)
    for c in range(NCORES):
        out[c * ns:(c + 1) * ns] = res.results[c]["outT"].transpose(0, 2, 1)
    return out
```
